# Optimizing a Trainium2 kernel written in Bass

```python
import math, functools
import jax, jax.numpy as jnp
from jax import lax
import numpy as np

D_MODEL = 1024
BATCH = 4
SEQ = 4096
DEPTH = 2

N_MIXERS = 2
N_A = (DEPTH + 1) // 2
N_B = DEPTH // 2
D_RNN = D_MODEL
RG_BLOCKS = 4
RG_BLOCK = D_RNN // RG_BLOCKS
RG_CONV = 4
RG_C = 8.0
GLA_HEADS = 4
GLA_DK = D_MODEL // 2 // GLA_HEADS
GLA_DV = D_MODEL // GLA_HEADS
GLA_RANK = 16
GLA_TAU = 16.0
GLA_CHUNK = 64
GLA_QK = GLA_HEADS * GLA_DK
GLA_IN = 2 * GLA_QK + 2 * D_MODEL + GLA_RANK
D_FF = ((8 * D_MODEL // 3 + 127) // 128) * 128
FFN_CONV = 3
EPS = 1e-6

kernel_name = "hybrid_rglru_gla_convffn_adaln"


def rmsnorm(x, g):
    x32 = x.astype(jnp.float32)
    y = x32 * lax.rsqrt(jnp.mean(x32 * x32, axis=-1, keepdims=True) + EPS)
    return (y * g.astype(jnp.float32)).astype(x.dtype)


def causal_dwconv(x, w, b):
    k_width = w.shape[0]
    s = x.shape[1]
    xp = jnp.pad(x, ((0, 0), (k_width - 1, 0), (0, 0)))
    y = b
    for k in range(k_width):
        y = y + xp[:, k:k + s] * w[k]
    return y


def _lru_combine(left, right):
    a1, b1 = left
    a2, b2 = right
    return a1 * a2, a2 * b1 + b2


def rglru_mixer(h, w_in, conv_w, conv_b, wa, ba, wx, bx, lam, w_out):
    bsz, s, _ = h.shape
    gate_br, x_br = jnp.split(h @ w_in, 2, axis=-1)
    x_br = causal_dwconv(x_br, conv_w, conv_b)
    xb = x_br.reshape(bsz, s, RG_BLOCKS, RG_BLOCK)
    r = jax.nn.sigmoid(jnp.einsum('bsgi,gij->bsgj', xb, wa).reshape(bsz, s, D_RNN) + ba)
    i_g = jax.nn.sigmoid(jnp.einsum('bsgi,gij->bsgj', xb, wx).reshape(bsz, s, D_RNN) + bx)
    log_a = -RG_C * r.astype(jnp.float32) * jax.nn.softplus(-lam.astype(jnp.float32))
    a = jnp.exp(log_a)
    mult = jnp.sqrt(-jnp.expm1(2.0 * log_a))
    u = mult * (i_g * x_br).astype(jnp.float32)
    _, hs = lax.associative_scan(_lru_combine, (a, u), axis=1)
    y = jax.nn.gelu(gate_br) * hs.astype(h.dtype)
    return y @ w_out


def gla_chunk_scan(q, k, v, g):
    n_c, bsz, nh, cl, dk = q.shape
    dv = v.shape[-1]
    mask = jnp.tril(jnp.ones((cl, cl), dtype=bool))[:, :, None]

    def step(state, inp):
        qc, kc, vc, gc = inp
        big_g = jnp.cumsum(gc, axis=2)
        o_inter = jnp.einsum('bhcd,bhde->bhce', qc * jnp.exp(big_g), state)
        diff = big_g[:, :, :, None, :] - big_g[:, :, None, :, :]
        decay = jnp.exp(jnp.where(mask, diff, -jnp.inf))
        attn = jnp.einsum('bhid,bhjd,bhijd->bhij', qc, kc, decay)
        o_intra = jnp.einsum('bhij,bhje->bhie', attn, vc)
        g_last = big_g[:, :, -1:, :]
        state = jnp.exp(g_last[:, :, 0, :])[..., None] * state + jnp.einsum(
            'bhjd,bhje->bhde', kc * jnp.exp(g_last - big_g), vc)
        return state, o_inter + o_intra

    state0 = jnp.zeros((bsz, nh, dk, dv), jnp.float32)
    _, o = lax.scan(step, state0, (q, k, v, g))
    return o


def gla_mixer(h, w_in, w_alpha, b_alpha, norm_g, w_out):
    bsz, s, _ = h.shape
    proj = h @ w_in
    q, k, v, r, z = jnp.split(proj, [GLA_QK, 2 * GLA_QK, 2 * GLA_QK + D_MODEL,
                                     2 * GLA_QK + 2 * D_MODEL], axis=-1)
    log_alpha = jax.nn.log_sigmoid((z @ w_alpha + b_alpha).astype(jnp.float32)) / GLA_TAU
    n_c = s // GLA_CHUNK

    def to_chunks(t, d):
        return t.reshape(bsz, n_c, GLA_CHUNK, GLA_HEADS, d).transpose(1, 0, 3, 2, 4).astype(jnp.float32)

    o = gla_chunk_scan(to_chunks(q * (GLA_DK ** -0.5), GLA_DK), to_chunks(k, GLA_DK),
                       to_chunks(v, GLA_DV), to_chunks(log_alpha, GLA_DK))
    o = o.transpose(1, 0, 3, 2, 4).reshape(bsz, s, GLA_HEADS, GLA_DV)
    o = rmsnorm(o, norm_g).reshape(bsz, s, D_MODEL).astype(h.dtype)
    return (o * jax.nn.silu(r)) @ w_out


def conv_ffn(h, w_up, conv_w, conv_b, w_down):
    u = causal_dwconv(h @ w_up, conv_w, conv_b)
    g, val = jnp.split(u, 2, axis=-1)
    return (jax.nn.gelu(g) * val) @ w_down


def setup_inputs(seed: int = 0) -> dict:
    key = jax.random.key(seed)
    ks = iter(jax.random.split(key, 32))

    def nrm(shape, scale):
        return jax.random.normal(next(ks), shape, jnp.float32) * scale

    d = D_MODEL
    x = nrm((BATCH, SEQ, d), 1.0)
    c = nrm((BATCH, d), 1.0)
    ada_w = nrm((DEPTH, d, 6 * d), d ** -0.5)
    ada_b = nrm((DEPTH, 6 * d), 0.01)
    norm_g = 1.0 + nrm((DEPTH, 4, d), 0.05)
    ffn_w_up = nrm((DEPTH, d, 2 * D_FF), d ** -0.5)
    ffn_conv_w = nrm((DEPTH, FFN_CONV, 2 * D_FF), FFN_CONV ** -0.5)
    ffn_conv_b = nrm((DEPTH, 2 * D_FF), 0.01)
    ffn_w_down = nrm((DEPTH, D_FF, d), D_FF ** -0.5)
    rg_w_in = nrm((N_A, d, 2 * D_RNN), d ** -0.5)
    rg_conv_w = nrm((N_A, RG_CONV, D_RNN), RG_CONV ** -0.5)
    rg_conv_b = nrm((N_A, D_RNN), 0.01)
    rg_wa = nrm((N_A, RG_BLOCKS, RG_BLOCK, RG_BLOCK), RG_BLOCK ** -0.5)
    rg_ba = nrm((N_A, D_RNN), 0.01)
    rg_wx = nrm((N_A, RG_BLOCKS, RG_BLOCK, RG_BLOCK), RG_BLOCK ** -0.5)
    rg_bx = nrm((N_A, D_RNN), 0.01)
    a_pow = jax.random.uniform(next(ks), (N_A, D_RNN), jnp.float32, 0.9, 0.999)
    a_base = a_pow ** (1.0 / RG_C)
    rg_lambda = jnp.log(a_base) - jnp.log1p(-a_base)
    rg_w_out = nrm((N_A, D_RNN, d), D_RNN ** -0.5)
    gla_w_in = nrm((N_B, d, GLA_IN), d ** -0.5)
    gla_w_alpha = nrm((N_B, GLA_RANK, GLA_QK), GLA_RANK ** -0.5)
    gla_b_alpha = nrm((N_B, GLA_QK), 0.01)
    gla_norm_g = 1.0 + nrm((N_B, GLA_DV), 0.05)
    gla_w_out = nrm((N_B, d, d), d ** -0.5)
    return {"x": x, "c": c, "ada_w": ada_w, "ada_b": ada_b, "norm_g": norm_g,
            "ffn_w_up": ffn_w_up, "ffn_conv_w": ffn_conv_w, "ffn_conv_b": ffn_conv_b,
            "ffn_w_down": ffn_w_down, "rg_w_in": rg_w_in, "rg_conv_w": rg_conv_w,
            "rg_conv_b": rg_conv_b, "rg_wa": rg_wa, "rg_ba": rg_ba, "rg_wx": rg_wx,
            "rg_bx": rg_bx, "rg_lambda": rg_lambda, "rg_w_out": rg_w_out,
            "gla_w_in": gla_w_in, "gla_w_alpha": gla_w_alpha, "gla_b_alpha": gla_b_alpha,
            "gla_norm_g": gla_norm_g, "gla_w_out": gla_w_out}


def reference(x, c, ada_w, ada_b, norm_g, ffn_w_up, ffn_conv_w, ffn_conv_b, ffn_w_down,
              rg_w_in, rg_conv_w, rg_conv_b, rg_wa, rg_ba, rg_wx, rg_bx, rg_lambda, rg_w_out,
              gla_w_in, gla_w_alpha, gla_b_alpha, gla_norm_g, gla_w_out):
    c_act = jax.nn.silu(c)
    for i in range(DEPTH):
        mod = (c_act @ ada_w[i] + ada_b[i])[:, None, :]
        sh_m, sc_m, gt_m, sh_f, sc_f, gt_f = jnp.split(mod, 6, axis=-1)
        j = i // N_MIXERS
        h = rmsnorm(x, norm_g[i, 0]) * (1.0 + sc_m) + sh_m
        if i % N_MIXERS == 0:
            y = rglru_mixer(h, rg_w_in[j], rg_conv_w[j], rg_conv_b[j], rg_wa[j], rg_ba[j],
                            rg_wx[j], rg_bx[j], rg_lambda[j], rg_w_out[j])
        else:
            y = gla_mixer(h, gla_w_in[j], gla_w_alpha[j], gla_b_alpha[j], gla_norm_g[j], gla_w_out[j])
        x = x + gt_m * rmsnorm(y, norm_g[i, 1])
        h = rmsnorm(x, norm_g[i, 2]) * (1.0 + sc_f) + sh_f
        y = conv_ffn(h, ffn_w_up[i], ffn_conv_w[i], ffn_conv_b[i], ffn_w_down[i])
        x = x + gt_f * rmsnorm(y, norm_g[i, 3])
    return x
```

```python
import os
from contextlib import ExitStack
import numpy as np
import concourse.bass as bass
import concourse.mybir as mybir
from concourse.bass_utils import run_bass_kernel_spmd

F32 = mybir.dt.float32
BF16 = mybir.dt.bfloat16
AF = mybir.ActivationFunctionType
ALU = mybir.AluOpType

D = 1024
S = 4096
NB = 4
DFF = 2816
KT = 8
FT = 22
SEG = 1024
TT = 512
NTT = SEG // TT
EPS = 1e-6
NCORES = 8


class Res:
    __slots__ = ("name", "w", "r")

    def __init__(self, name):
        self.name = name
        self.w = None
        self.r = []


class Sched:
    ENGS = ["pe", "act", "dve", "pool", "sp"]

    def __init__(self, nc):
        self.nc = nc
        self.q = {e: [] for e in self.ENGS}
        self.cnt = {e: 0 for e in self.ENGS}
        self.dcnt = {}
        self.seen = {e: {} for e in self.ENGS}
        self.floor = {e: {} for e in self.ENGS}

    def _deps(self, eng, reads, writes):
        need = {}

        def add(tok):
            if tok is None:
                return
            k, v = tok
            if k == eng and eng == "pe":
                return
            if need.get(k, 0) < v:
                need[k] = v

        for r in reads:
            add(r.w)
        for w in writes:
            add(w.w)
            for t in w.r:
                add(t)
        for k, v in self.floor[eng].items():
            if not (k == eng and eng == "pe"):
                if need.get(k, 0) < v:
                    need[k] = v
        self.floor[eng] = {}
        waits = []
        seen = self.seen[eng]
        for k, v in need.items():
            if seen.get(k, 0) < v:
                seen[k] = v
                waits.append((k, v))
        return waits

    def op(self, eng, fn, reads=(), writes=(), inc=True):
        waits = self._deps(eng, reads, writes)
        if inc:
            self.cnt[eng] += 1
            tok = (eng, self.cnt[eng])
        else:
            tok = (eng, self.cnt[eng] + 1)
        self.q[eng].append((waits, fn, (eng, 1) if inc else None))
        for r in reads:
            r.r.append(tok)
        for w in writes:
            w.w = tok
            w.r = []
        return tok

    def dma(self, qeng, fn, reads=(), writes=(), key="x"):
        waits = self._deps(qeng, reads, writes)
        k = "d:" + key
        self.dcnt[k] = self.dcnt.get(k, 0) + 16
        tok = (k, self.dcnt[k])
        self.q[qeng].append((waits, fn, (k, 16)))
        for r in reads:
            r.r.append(tok)
        for w in writes:
            w.w = tok
            w.r = []
        return tok

    def barrier(self, engs=("pe", "act", "dve"), final=False):
        snap = {e: self.cnt[e] for e in self.ENGS if self.cnt[e] > 0}
        snap.update({k: v for k, v in self.dcnt.items() if final or not k.startswith("d:w")})
        for e in engs:
            f = self.floor[e]
            for k, v in snap.items():
                if f.get(k, 0) < v:
                    f[k] = v

    def emit(self):
        nc = self.nc
        self.barrier(engs=("sp",), final=True)
        waits = self._deps("sp", (), ())
        self.q["sp"].append((waits, None, None))
        keys = list(self.ENGS) + list(self.dcnt.keys())
        with ExitStack() as st:
            sem = {}
            for i, k in enumerate(keys):
                sem[k] = st.enter_context(nc.semaphore("s%d" % i))
            block = st.enter_context(nc.Block())

            def run(ename, eobj):
                for waits, fn, inc in self.q[ename]:
                    for k, v in waits:
                        eobj.wait_ge(sem[k], v)
                    if fn is None:
                        continue
                    ins = fn(eobj)
                    if inc is not None:
                        ins.then_inc(sem[inc[0]], inc[1])

            block.tensor(lambda e: run("pe", e))
            block.scalar(lambda e: run("act", e))
            block.vector(lambda e: run("dve", e))
            block.gpsimd(lambda e: run("pool", e))
            block.sync(lambda e: run("sp", e))


def _prod(xs):
    r = 1
    for x in xs:
        r *= x
    return r


class Arena:
    def __init__(self, t_f32, nwords):
        self.t = t_f32
        self.nwords = nwords

    def view(self, off_bytes, shape, dtype):
        assert off_bytes % 4 == 0
        n = _prod(shape[1:])
        esz = 4 if dtype == F32 else 2
        nb = n * esz
        assert nb % 4 == 0
        assert off_bytes + nb <= self.nwords * 4, (off_bytes, nb, self.nwords * 4)
        base = self.t[0:shape[0], off_bytes // 4:(off_bytes + nb) // 4]
        ap = base if dtype == F32 else base.bitcast(dtype)
        if len(shape) == 3:
            ap = ap.rearrange("p (a b) -> p a b", a=shape[1])
        elif len(shape) == 4:
            ap = ap.rearrange("p (a b c) -> p a b c", a=shape[1], b=shape[2])
        return ap


def lay_lhsT(W):
    K, M = W.shape
    return np.ascontiguousarray(W.reshape(K // 128, 128, M // 128, 128).transpose(2, 1, 0, 3))


def lay_vec(v):
    return np.ascontiguousarray(v.reshape(-1, 128).T)


def small_layout():
    off = {}
    c = 0

    def add(name, n):
        nonlocal c
        off[name] = (c, n)
        c += n

    add("c", 8)
    add("flag", 1)
    for i in range(2):
        for j in range(4):
            add("ng%d_%d" % (i, j), 8)
        add("adab%d" % i, 48)
        add("fcw%d" % i, 44 * 3)
        add("fcb%d" % i, 44)
    add("rgcw", 8 * 4)
    add("rgcb", 8)
    add("rgba", 8)
    add("rgbx", 8)
    add("rglam", 8)
    add("glab", 4)
    add("glag", 2)
    return off, c


SM_OFF, NSMALL = small_layout()


def pack_small(inp, b, flag=1.0):
    sm = np.zeros((128, NSMALL), np.float32)

    def put(name, arr):
        o, n = SM_OFF[name]
        arr = np.asarray(arr, np.float32).reshape(128, n)
        sm[:, o:o + n] = arr

    put("c", lay_vec(inp["c"][b]))
    put("flag", np.full((128, 1), flag, np.float32))
    for i in range(2):
        for j in range(4):
            put("ng%d_%d" % (i, j), lay_vec(inp["norm_g"][i, j]))
        put("adab%d" % i, lay_vec(inp["ada_b"][i]))
        cw = inp["ffn_conv_w"][i]
        put("fcw%d" % i, np.stack([lay_vec(cw[k]) for k in range(3)], axis=-1))
        put("fcb%d" % i, lay_vec(inp["ffn_conv_b"][i]))
    rcw = inp["rg_conv_w"][0]
    put("rgcw", np.stack([lay_vec(rcw[k]) for k in range(4)], axis=-1))
    put("rgcb", lay_vec(inp["rg_conv_b"][0]))
    put("rgba", lay_vec(inp["rg_ba"][0]))
    put("rgbx", lay_vec(inp["rg_bx"][0]))
    put("rglam", lay_vec(inp["rg_lambda"][0]))
    put("glab", lay_vec(inp["gla_b_alpha"][0]))
    put("glag", lay_vec(inp["gla_norm_g"][0]))
    return sm


def prep_weights(inp):
    w = {}
    w["ada"] = np.stack([lay_lhsT(inp["ada_w"][i]) for i in range(2)])
    w["wup"] = np.stack([lay_lhsT(inp["ffn_w_up"][i]) for i in range(2)])
    w["wdn"] = np.stack([lay_lhsT(inp["ffn_w_down"][i]) for i in range(2)])
    w["rgin"] = lay_lhsT(inp["rg_w_in"][0])
    wa = inp["rg_wa"][0]
    wx = inp["rg_wx"][0]
    w["rgwa"] = np.concatenate([lay_lhsT(wa[g]) for g in range(4)], axis=0)
    w["rgwx"] = np.concatenate([lay_lhsT(wx[g]) for g in range(4)], axis=0)
    w["rgout"] = lay_lhsT(inp["rg_w_out"][0])
    gw = inp["gla_w_in"][0]
    w["glaqk"] = lay_lhsT(gw[:, 0:1024])
    w["glar"] = lay_lhsT(gw[:, 2048:3072])
    w["glav"] = np.ascontiguousarray(gw[:, 1024:2048].reshape(8, 128, 1024).transpose(1, 0, 2))
    w["glaz"] = np.ascontiguousarray(gw[:, 3072:3088].reshape(8, 128, 16).transpose(1, 0, 2))
    w["walpha"] = np.ascontiguousarray(inp["gla_w_alpha"][0])
    w["glaout"] = lay_lhsT(inp["gla_w_out"][0])
    return w


W_SHAPES = {
    "ada": [2, 48, 128, 8, 128], "wup": [2, 44, 128, 8, 128], "wdn": [2, 8, 128, 22, 128],
    "rgin": [16, 128, 8, 128], "rgwa": [8, 128, 2, 128], "rgwx": [8, 128, 2, 128],
    "rgout": [8, 128, 8, 128], "glaqk": [8, 128, 8, 128], "glar": [8, 128, 8, 128],
    "glav": [128, 8, 1024], "glaz": [128, 8, 16], "walpha": [16, 512], "glaout": [8, 128, 8, 128],
}


class Builder:
    def __init__(self, nseg, stages, npre=0):
        self.nseg = nseg
        self.npre = npre
        self.stages = stages
        self.nc = bass.Bass("TRN2", target_bir_lowering=False)
        self.POOL_M = (3, 7)
        self.sc = Sched(self.nc)

    def col(self, name, j=None, n=None):
        o, w = SM_OFF[name]
        if j is None:
            return self.small[:, o:o + w]
        return self.small[:, o + j:o + j + (1 if n is None else n)]

    def newc(self, name, n):
        o = self.cc_off
        self.cc_off += n
        assert self.cc_off <= self.cc_n
        self.cc_map[name] = (o, n)
        return self.cc[:, o:o + n]

    def cst(self, name, j=None):
        o, n = self.cc_map[name]
        if j is None:
            return self.cc[:, o:o + n]
        return self.cc[:, o + j:o + j + 1]

    def mm_bank(self):
        i = self.mm_i % 8
        self.mm_i += 1
        while i in getattr(self, "mm_excl", ()):
            i = self.mm_i % 8
            self.mm_i += 1
        return self.psall[:, i * 512:(i + 1) * 512], self.bank_res[i]

    def wload(self, src, shape):
        i = self.w_i % self.NW
        self.w_i += 1
        n = _prod(shape[1:])
        assert n <= self.WSLOT
        v = self.wring[0:shape[0], i, 0:n]
        if len(shape) == 3:
            v = v.rearrange("p (a b) -> p a b", a=shape[1])
        elif len(shape) == 4:
            v = v.rearrange("p (a b c) -> p a b c", a=shape[1], b=shape[2])
        res = self.w_res[i]
        self.sc.dma("pool", lambda e, v=v, src=src: e.dma_start(out=v, in_=src),
                    reads=(), writes=[res], key="w%d" % i)
        return v, res

    def mm_group(self, out_ap, out_res, pairs, reads):
        n = len(pairs)
        for i, (l, r) in enumerate(pairs):
            self.sc.op("pe",
                       lambda e, l=l, r=r, i=i: e.matmul(out_ap, l, r, start=(i == 0), stop=(i == n - 1)),
                       reads=reads if i == 0 else (), writes=[out_res] if i == 0 else (),
                       inc=(i == n - 1))

    def tmp(self, name, shape, dtype, nbuf):
        key = name
        if key not in self.tmps:
            n = _prod(shape[1:]) * (4 if dtype == F32 else 2)
            n = (n + 31) // 32 * 32
            bufs = []
            for b in range(nbuf):
                v = self.arena.view(self.tmp_off, shape, dtype)
                self.tmp_off += n
                assert self.tmp_off <= self.tmp_end, (name, self.tmp_off, self.tmp_end)
                bufs.append((v, Res("%s%d" % (name, b))))
            self.tmps[key] = [bufs, 0]
        ent = self.tmps[key]
        b = ent[0][ent[1] % len(ent[0])]
        ent[1] += 1
        return b

    def ntmp(self, name, shape, dtype, nbuf):
        if name not in self.ntmps:
            n = _prod(shape[1:]) * (4 if dtype == F32 else 2)
            n = (n + 31) // 32 * 32
            bufs = []
            for b in range(nbuf):
                v = self.arena.view(self.ntmp_off, shape, dtype)
                self.ntmp_off += n
                assert self.ntmp_off <= self.arena_bytes, (name, self.ntmp_off)
                bufs.append((v, Res("n%s%d" % (name, b))))
            self.ntmps[name] = [bufs, 0]
        ent = self.ntmps[name]
        b = ent[0][ent[1] % len(ent[0])]
        ent[1] += 1
        return b

    def reset_tmps(self, start, end):
        end = min(end, self.NT0)
        self.tmps = {}
        self.tmp_off = start
        self.tmp_end = end

    def rstd_from_sq(self, sq_aps, sq_res):
        ps, pres = self.mm_bank()
        self.mm_group(ps, pres, [(self.ones, a) for a in sq_aps], reads=list(sq_res))
        n = len(sq_aps) * 128
        rstd, rres = self.ntmp("rstd", [128, TT], F32, 2)
        self.sc.op("act", lambda e: e.activation(rstd, ps, AF.Ln, bias=self.cst("eps"), scale=1.0 / n),
                   reads=[pres], writes=[rres])
        self.sc.op("act", lambda e: e.activation(rstd, rstd, AF.Exp, scale=-0.5),
                   reads=[rres], writes=[rres])
        return rstd, rres

    def prenorm(self, gmod, shift, tts=None):
        sc = self.sc
        for tt in (range(NTT) if tts is None else tts):
            ts = slice(tt * TT, (tt + 1) * TT)
            sqs, sres = [], []
            for k in range(KT):
                sq, sqr = self.ntmp("sq", [128, TT], BF16, 8)
                x = self.xres[:, k, ts]
                sc.op("act", lambda e, sq=sq, x=x: e.activation(sq, x, AF.Square),
                      reads=[self.x_res[k][tt]], writes=[sqr])
                sqs.append(sq)
                sres.append(sqr)
            rstd, rres = self.rstd_from_sq(sqs, sres)
            for k in range(KT):
                t, tr = self.ntmp("nt", [128, TT], F32, 3)
                x = self.xres[:, k, ts]
                sc.op("dve", lambda e, t=t, x=x, rstd=rstd: e.tensor_tensor(t, x, rstd, ALU.mult),
                      reads=[self.x_res[k][tt], rres], writes=[tr])
                h = self.hT[:, k, ts]
                sc.op("act", lambda e, h=h, t=t, k=k: e.activation(
                    h, t, AF.Identity, bias=self.cst(shift, k), scale=self.cst(gmod, k)),
                    reads=[tr], writes=[self.h_res[k][tt]])

    def postnorm_residual(self, gg, tts=None, to_y=False, after_tt=None):
        sc = self.sc
        for tt in (range(NTT) if tts is None else tts):
            ts = slice(tt * TT, (tt + 1) * TT)
            rstd, rres = self.rstd_from_sq([self.ysq[:, m, ts] for m in range(KT)],
                                           [self.ysq_res[m][tt] for m in range(KT)])
            for m in range(KT):
                y = self.ymix[:, m, ts]
                x = self.xres[:, m, ts]
                eng = "pool" if m in self.POOL_M else "dve"
                sc.op(eng, lambda e, y=y, rstd=rstd: e.tensor_tensor(y, y, rstd, ALU.mult),
                      reads=[self.y_res[m][tt], rres], writes=[self.y_res[m][tt]])
                if to_y:
                    sc.op(eng, lambda e, y=y, x=x: e.tensor_tensor(y, x, y, ALU.add),
                          reads=[self.y_res[m][tt], self.x_res[m][tt]], writes=[self.y_res[m][tt]])
                else:
                    sc.op(eng, lambda e, y=y, x=x: e.tensor_tensor(x, x, y, ALU.add),
                          reads=[self.y_res[m][tt], self.x_res[m][tt]], writes=[self.x_res[m][tt]])
            if after_tt is not None:
                after_tt(tt)

    def outproj(self, wsrc, nk, act_ap, act_res, gg, fused=False, tts=None, to_y=False, after_tt=None):
        sc = self.sc

        def group(wv_m, wr, m, tt):
            ts = slice(tt * TT, (tt + 1) * TT)
            ps, pres = self.mm_bank()
            self.mm_group(ps, pres, [(wv_m[:, k, :], act_ap(k, ts)) for k in range(nk)],
                          reads=[wr] + [act_res(k, tt) for k in range(nk)])
            y = self.ymix[:, m, ts]
            q = self.ysq[:, m, ts]
            sc.op("act", lambda e, y=y, ps=ps, m=m: e.activation(y, ps, AF.Copy, scale=self.cst(gg, m)),
                  reads=[pres], writes=[self.y_res[m][tt]])
            sc.op("act", lambda e, q=q, ps=ps: e.activation(q, ps, AF.Square),
                  reads=[pres], writes=[self.ysq_res[m][tt]])

        if fused and nk == FT:
            ws = []
            for m in range(6):
                wv, wr = self.wload(wsrc[m], [128, nk, 128])
                ws.append((wv, [wr]))
            sc.barrier(engs=("pool",))
            ex0 = self.arena.view(94208, [128, nk, 128], BF16)
            ex0_res = Res("wex0")
            ex1 = self.arena.view(self.NT0, [128, nk, 128], BF16)
            ex1_res = [b[1] for b in self.ntmps["sq"][0][0:6]]
            sc.dma("pool", lambda e: e.dma_start(out=ex0, in_=wsrc[6]), writes=[ex0_res], key="wex0")
            sc.dma("pool", lambda e: e.dma_start(out=ex1, in_=wsrc[7]), writes=ex1_res, key="wex1")
            ws.append((ex0, [ex0_res]))
            ws.append((ex1, ex1_res))
            for tt in (range(NTT) if tts is None else tts):
                ts = slice(tt * TT, (tt + 1) * TT)
                for m in range(KT):
                    wv, wrl = ws[m]
                    ps, pres = self.mm_bank()
                    self.mm_group(ps, pres, [(wv[:, k, :], act_ap(k, ts)) for k in range(nk)],
                                  reads=list(wrl) + [act_res(k, tt) for k in range(nk)])
                    y = self.ymix[:, m, ts]
                    q = self.ysq[:, m, ts]
                    sc.op("act", lambda e, y=y, ps=ps, m=m: e.activation(y, ps, AF.Copy, scale=self.cst(gg, m)),
                          reads=[pres], writes=[self.y_res[m][tt]])
                    sc.op("act", lambda e, q=q, ps=ps: e.activation(q, ps, AF.Square),
                          reads=[pres], writes=[self.ysq_res[m][tt]])
                self.postnorm_residual(gg, tts=[tt], to_y=to_y, after_tt=after_tt)
            return
        if fused:
            ws = []
            for m2 in range(KT // 2):
                wv, wr = self.wload(wsrc[2 * m2:2 * m2 + 2].rearrange("c p k m -> p c k m"), [128, 2, nk, 128])
                ws.append((wv, wr))
            for tt in (range(NTT) if tts is None else tts):
                for m in range(KT):
                    wv, wr = ws[m // 2]
                    group(wv[:, m % 2], wr, m, tt)
                self.postnorm_residual(gg, tts=[tt], after_tt=after_tt)
            return
        for m in range(KT):
            wv, wr = self.wload(wsrc[m], [128, nk, 128])
            for tt in range(NTT):
                group(wv, wr, m, tt)

    def ffn(self, li, halo_only=False, final=False, after_tt=None):
        sc = self.sc
        A = self.arena
        self.hT = A.view(0, [128, KT, SEG], BF16)
        self.ysq = self.hT
        aT = A.view(16384, [128, FT, SEG], BF16)
        a_res = [[Res("a%d_%d" % (f, tt)) for tt in range(NTT)] for f in range(FT)]
        self.ymix = A.view(61440, [128, KT, SEG], F32)
        self.reset_tmps(61440, self.arena_bytes)
        want = [NTT - 1] if halo_only else list(range(NTT))
        done = getattr(self, "pre_done", set())
        self.prenorm("gmod_f%d" % li, "sh_f%d" % li, tts=[t for t in want if (li, t) not in done])
        self.pre_done = set()
        sc.barrier(engs=("act", "dve"))
        wup = self.W["wup"]
        fcw = SM_OFF["fcw%d" % li][0]
        fcb = SM_OFF["fcb%d" % li][0]
        if halo_only:
            ts = slice(SEG - 128, SEG)
            for c in range(FT):
                wv, wr = self.wload(wup[li, c:c + FT + 1:FT].rearrange("c p k m -> p c k m"), [128, 2, KT, 128])
                for gv in range(2):
                    ch = c + gv * FT
                    ps, pres = self.mm_bank()
                    self.mm_group(ps[:, 0:128], pres, [(wv[:, gv, k, :], self.hT[:, k, ts]) for k in range(KT)],
                                  reads=[wr] + [self.h_res[k][NTT - 1] for k in range(KT)])
                    halo = self.fhalo[:, li, ch, :]
                    sc.op("act", lambda e, halo=halo, ps=ps: e.activation(halo, ps[:, 126:128], AF.Copy),
                          reads=[pres], writes=[self.fhalo_res[li][ch]])
            sc.barrier()
            return
        pend = []

        def f2_tail(c, accs):
            (ag, agr), (av, avr) = accs
            sc.op("act", lambda e, ag=ag: e.activation(ag, ag, AF.Gelu_apprx_tanh),
                  reads=[agr], writes=[agr])
            sc.op("dve", lambda e, ag=ag, av=av, c=c: e.tensor_tensor(aT[:, c, :], ag, av, ALU.mult),
                  reads=[agr, avr], writes=[a_res[c][0], a_res[c][1]])

        for c in range(FT):
            self.fill(1)
            wv, wr = self.wload(wup[li, c:c + FT + 1:FT].rearrange("c p k m -> p c k m"), [128, 2, KT, 128])
            accs = []
            for gv in range(2):
                ch = c + gv * FT
                ub, ur = self.tmp("ub", [128, SEG + 2], F32, 4)
                acc, ar = self.tmp("acc", [128, SEG], F32, 5)
                halo = self.fhalo[:, li, ch, :]
                hres = self.fhalo_res[li][ch]
                sc.op("act", lambda e, ub=ub, halo=halo: e.activation(ub[:, 0:2], halo, AF.Copy),
                      reads=[hres], writes=[ur])
                for tt in range(NTT):
                    ts = slice(tt * TT, (tt + 1) * TT)
                    ps, pres = self.mm_bank()
                    self.mm_group(ps, pres, [(wv[:, gv, k, :], self.hT[:, k, ts]) for k in range(KT)],
                                  reads=[wr] + [self.h_res[k][tt] for k in range(KT)])
                    sc.op("act", lambda e, ub=ub, ps=ps, tt=tt: e.activation(
                        ub[:, 2 + tt * TT:2 + (tt + 1) * TT], ps, AF.Copy),
                        reads=[pres], writes=[ur])
                    sc.op("act", lambda e, acc=acc, ps=ps, ts=ts, ch=ch: e.activation(
                        acc[:, ts], ps, AF.Identity,
                        bias=self.small[:, fcb + ch:fcb + ch + 1],
                        scale=self.small[:, fcw + ch * 3 + 2:fcw + ch * 3 + 3]),
                        reads=[pres], writes=[ar])
                sc.op("dve", lambda e, acc=acc, ub=ub, ch=ch: e.scalar_tensor_tensor(
                    acc, ub[:, 1:1 + SEG], self.small[:, fcw + ch * 3 + 1:fcw + ch * 3 + 2], acc,
                    ALU.mult, ALU.add), reads=[ur, ar], writes=[ar])
                sc.op("dve", lambda e, acc=acc, ub=ub, ch=ch: e.scalar_tensor_tensor(
                    acc, ub[:, 0:SEG], self.small[:, fcw + ch * 3:fcw + ch * 3 + 1], acc,
                    ALU.mult, ALU.add), reads=[ur, ar], writes=[ar])
                sc.op("dve", lambda e, ub=ub, halo=halo: e.tensor_copy(halo, ub[:, SEG:SEG + 2]),
                      reads=[ur], writes=[hres])
                accs.append((acc, ar))
            pend.append((c, accs))
            if len(pend) > 1:
                f2_tail(*pend.pop(0))
        while pend:
            f2_tail(*pend.pop(0))
        sc.barrier()
        self.reset_tmps(16384, 61440)
        self.outproj(self.W["wdn"][li], FT, lambda k, ts: aT[:, k, ts], lambda k, tt: a_res[k][tt],
                     gg="gg_f%d" % li, fused=True, to_y=final, after_tt=after_tt)

    def rglru(self, after_tt=None):
        sc = self.sc
        A = self.arena
        self.hT = A.view(0, [128, KT, SEG], BF16)
        self.ysq = self.hT
        gy = A.view(16384, [128, KT, SEG], BF16)
        gy_res = [[Res("gy%d_%d" % (f, tt)) for tt in range(NTT)] for f in range(KT)]
        self.ymix = A.view(61440, [128, KT, SEG], F32)
        self.reset_tmps(32768, self.arena_bytes)
        self.prenorm("gmod_m0", "sh_m0")
        sc.barrier(engs=("act", "dve"))
        rgin = self.W["rgin"]
        def gate(f):
            wv, wr = self.wload(rgin[f], [128, KT, 128])
            for tt in range(NTT):
                ts = slice(tt * TT, (tt + 1) * TT)
                ps, pres = self.mm_bank()
                self.mm_group(ps, pres, [(wv[:, k, :], self.hT[:, k, ts]) for k in range(KT)],
                              reads=[wr] + [self.h_res[k][tt] for k in range(KT)])
                sc.op("act", lambda e, f=f, ts=ts, ps=ps: e.activation(gy[:, f, ts], ps, AF.Gelu_apprx_tanh),
                      reads=[pres], writes=[gy_res[f][tt]])

        cw = SM_OFF["rgcw"][0]

        def stageA(j):
            xcs = []
            for f in (2 * j, 2 * j + 1):
                wv, wr = self.wload(rgin[8 + f], [128, KT, 128])
                xb, xbr = self.tmp("xb", [128, SEG + 3], F32, 4)
                xc, xcr = self.tmp("xc", [128, SEG], F32, 2)
                xcb, xcbr = self.tmp("xcb", [128, SEG], BF16, 4)
                halo = self.rhalo[:, f, :]
                hres = self.rhalo_res[f]
                sc.op("act", lambda e, xb=xb, halo=halo: e.activation(xb[:, 0:3], halo, AF.Copy),
                      reads=[hres], writes=[xbr])
                for tt in range(NTT):
                    ts = slice(tt * TT, (tt + 1) * TT)
                    ps, pres = self.mm_bank()
                    self.mm_group(ps, pres, [(wv[:, k, :], self.hT[:, k, ts]) for k in range(KT)],
                                  reads=[wr] + [self.h_res[k][tt] for k in range(KT)])
                    sc.op("act", lambda e, xb=xb, ps=ps, tt=tt: e.activation(
                        xb[:, 3 + tt * TT:3 + (tt + 1) * TT], ps, AF.Copy),
                        reads=[pres], writes=[xbr])
                sc.op("act", lambda e, xb=xb, halo=halo: e.activation(halo, xb[:, SEG:SEG + 3], AF.Copy),
                      reads=[xbr], writes=[hres])
                sc.op("dve", lambda e, xc=xc, xb=xb, f=f: e.tensor_scalar(
                    xc, xb[:, 3:3 + SEG], self.small[:, cw + f * 4 + 3:cw + f * 4 + 4], self.col("rgcb", f),
                    ALU.mult, ALU.add), reads=[xbr], writes=[xcr])
                for kk in (2, 1):
                    sc.op("dve", lambda e, xc=xc, xb=xb, f=f, kk=kk: e.scalar_tensor_tensor(
                        xc, xb[:, kk:kk + SEG], self.small[:, cw + f * 4 + kk:cw + f * 4 + kk + 1], xc,
                        ALU.mult, ALU.add), reads=[xbr, xcr], writes=[xcr])
                sc.op("dve", lambda e, xc=xc, xcb=xcb, xb=xb, f=f: e.scalar_tensor_tensor(
                    xcb, xb[:, 0:SEG], self.small[:, cw + f * 4:cw + f * 4 + 1], xc,
                    ALU.mult, ALU.add), reads=[xbr, xcr], writes=[xcbr])
                xcs.append((xcb, xcbr))
            return xcs

        def stageB(j, xcs):
            wa, war = self.wload(self.W["rgwa"][2 * j:2 * j + 2].rearrange("c p k m -> p c k m"), [128, 2, 2, 128])
            wx, wxr = self.wload(self.W["rgwx"][2 * j:2 * j + 2].rearrange("c p k m -> p c k m"), [128, 2, 2, 128])
            ths = []
            for fi in range(2):
                f = 2 * j + fi
                tha, thar = self.tmp("tha", [128, SEG], F32, 2)
                thx, thxr = self.tmp("thx", [128, SEG], F32, 2)
                for (wv, wr, th, thr, bname) in ((wa, war, tha, thar, "hba"), (wx, wxr, thx, thxr, "hbx")):
                    for tt in range(NTT):
                        ts = slice(tt * TT, (tt + 1) * TT)
                        ps, pres = self.mm_bank()
                        self.mm_group(ps, pres, [(wv[:, fi, k, :], xcs[k][0][:, ts]) for k in range(2)],
                                      reads=[wr, xcs[0][1], xcs[1][1]])
                        sc.op("act", lambda e, th=th, ts=ts, ps=ps, bname=bname, f=f: e.activation(
                            th[:, ts], ps, AF.Tanh, bias=self.cst(bname, f), scale=0.5),
                            reads=[pres], writes=[thr])
                ths.append((tha, thar, thx, thxr))
            return ths

        def stageC(j, xcs, ths):
            bufs = []
            for fi in range(2):
                f = 2 * j + fi
                tha, thar, thx, thxr = ths[fi]
                av, avr = self.tmp("av", [128, SEG], F32, 2)
                a2, a2r = self.tmp("a2", [128, SEG], F32, 2)
                sc.op("act", lambda e, av=av, tha=tha, f=f: e.activation(
                    av, tha, AF.Exp, bias=self.cst("chalf", f), scale=self.cst("chalf", f)),
                    reads=[thar], writes=[avr])
                sc.op("act", lambda e, a2=a2, tha=tha, f=f: e.activation(
                    a2, tha, AF.Exp, bias=self.cst("clam", f), scale=self.cst("clam", f)),
                    reads=[thar], writes=[a2r])
                xcb, xcbr = xcs[fi]
                sc.op("dve", lambda e, thx=thx, xcb=xcb: e.scalar_tensor_tensor(
                    thx, thx, 1.0, xcb, ALU.add, ALU.mult), reads=[thxr, xcbr], writes=[thxr])
                bufs.append((av, avr, a2, a2r))
            for fi in range(2):
                av, avr, a2, a2r = bufs[fi]
                sc.op("dve", lambda e, a2=a2: e.tensor_scalar(a2, a2, 1.0, -1.0, ALU.min, ALU.mult),
                      reads=[a2r], writes=[a2r])
                sc.op("act", lambda e, a2=a2: e.activation(a2, a2, AF.Sqrt, bias=self.cst("one"), scale=1.0),
                      reads=[a2r], writes=[a2r])
            for fi in range(2):
                f = 2 * j + fi
                tha, thar, thx, thxr = ths[fi]
                av, avr, a2, a2r = bufs[fi]
                sc.op("dve", lambda e, thx=thx, a2=a2: e.scalar_tensor_tensor(
                    thx, thx, 0.5, a2, ALU.mult, ALU.mult), reads=[thxr, a2r], writes=[thxr])
                st = self.rstate[:, f:f + 1]
                sc.op("dve", lambda e, tha=tha, av=av, thx=thx, st=st: e.tensor_tensor_scan(
                    tha, av, thx, st, ALU.mult, ALU.add),
                    reads=[avr, thxr, self.rstate_res[f]], writes=[thar])
                sc.op("dve", lambda e, tha=tha, st=st: e.tensor_copy(st, tha[:, SEG - 1:SEG]),
                      reads=[thar], writes=[self.rstate_res[f]])
                sc.op("dve", lambda e, f=f, tha=tha: e.tensor_tensor(gy[:, f, :], gy[:, f, :], tha, ALU.mult),
                      reads=[thar, gy_res[f][0], gy_res[f][1]], writes=[gy_res[f][0], gy_res[f][1]])

        xcs_all = {}
        xcs_all[0] = stageA(0)
        for j in range(4):
            self.fill(1)
            if j + 1 < 4:
                xcs_all[j + 1] = stageA(j + 1)
            self.fill(1)
            ths = stageB(j, xcs_all[j])
            gate(2 * j)
            gate(2 * j + 1)
            if j % 2 == 1:
                self.fill(1)
            stageC(j, xcs_all[j], ths)
        self.flush()
        sc.barrier()
        self.reset_tmps(32768, 61440)
        self.outproj(self.W["rgout"], KT, lambda k, ts: gy[:, k, ts], lambda k, tt: gy_res[k][tt], gg="gg_m0", fused=True, after_tt=after_tt)

    def gla(self, state_only=False, otts=None, after_tt=None):
        sc = self.sc
        A = self.arena
        self.hT = A.view(0, [128, KT, SEG], BF16)
        self.ysq = self.hT
        qk = A.view(16384, [128, 8, SEG], BF16)
        qk_res = [[Res("qk%d_%d" % (f, tt)) for tt in range(NTT)] for f in range(8)]
        sr = A.view(32768, [128, KT, SEG], BF16)
        sr_res = [[Res("sr%d_%d" % (f, tt)) for tt in range(NTT)] for f in range(KT)]
        vtok = A.view(49152, [128, 8, D], BF16)
        v_res = [[Res("v%d_%d" % (j, h)) for h in range(4)] for j in range(8)]
        oT = A.view(65536, [128, KT, SEG], F32)
        o_res = [[Res("o%d_%d" % (f, j)) for j in range(8)] for f in range(KT)]
        self.ymix = A.view(61440, [128, KT, SEG], F32)
        T0 = 98304
        self.reset_tmps(65536, self.arena_bytes)
        self.prenorm("gmod_m1", "sh_m1")
        sc.barrier(engs=("act", "dve"))
        hres_all = lambda tt: [self.h_res[k][tt] for k in range(KT)]
        if otts is None:
            otts = list(range(NTT))
        if state_only:
            otts = []
        for f in range(0 if otts else KT, KT):
            wv, wr = self.wload(self.W["glar"][f], [128, KT, 128])
            for tt in otts:
                ts = slice(tt * TT, (tt + 1) * TT)
                ps, pres = self.mm_bank()
                self.mm_group(ps, pres, [(wv[:, k, :], self.hT[:, k, ts]) for k in range(KT)],
                              reads=[wr] + hres_all(tt))
                sc.op("act", lambda e, f=f, ts=ts, ps=ps: e.activation(sr[:, f, ts], ps, AF.Silu),
                      reads=[pres], writes=[sr_res[f][tt]])
        stop = int(os.environ.get("GLA_STOP", "9"))
        if stop <= 1:
            return
        wz, wzr = self.wload(self.W["glaz"], [128, KT, 16])
        zT, zr = self.tmp("zT", [16, SEG], BF16, 1)
        for tt in range(NTT):
            ts = slice(tt * TT, (tt + 1) * TT)
            ps, pres = self.mm_bank()
            self.mm_group(ps[0:16, :], pres, [(wz[:, k, :], self.hT[:, k, ts]) for k in range(KT)],
                          reads=[wzr] + hres_all(tt))
            sc.op("act", lambda e, ts=ts, ps=ps: e.activation(zT[:, ts], ps[0:16, :], AF.Copy),
                  reads=[pres], writes=[zr])
        if stop <= 2:
            return
        for h in range(4):
            lsp, lr = self.tmp("lsp", [128, SEG], F32, 1)
            Lc, Lr = self.tmp("Lc", [128, SEG], F32, 1)
            eG, eGr = self.tmp("eG", [128, SEG], F32, 1)
            enG, enGr = self.tmp("enG", [128, SEG], F32, 1)
            for tt in range(NTT):
                ts = slice(tt * TT, (tt + 1) * TT)
                ps, pres = self.mm_bank()
                self.mm_group(ps, pres, [(self.wal[:, h * 128:(h + 1) * 128], zT[:, ts])], reads=[zr, self.wal_res])
                sc.op("act", lambda e, lsp=lsp, ts=ts, ps=ps, h=h: e.activation(
                    lsp[:, ts], ps, AF.Exp, bias=self.cst("nglab", h), scale=-1.0),
                    reads=[pres], writes=[lr])
            sc.op("act", lambda e, lsp=lsp: e.activation(lsp, lsp, AF.Ln, bias=1.0),
                  reads=[lr], writes=[lr])
            sc.op("dve", lambda e, Lc=Lc, lsp=lsp: e.tensor_tensor_scan(
                Lc, self.rmask, lsp, 0.0, ALU.mult, ALU.add), reads=[lr, self.mask_res], writes=[Lr])
            sc.op("act", lambda e, eG=eG, Lc=Lc: e.activation(eG, Lc, AF.Exp, scale=-1.0 / 16.0),
                  reads=[Lr], writes=[eGr])
            sc.op("act", lambda e, enG=enG, Lc=Lc: e.activation(enG, Lc, AF.Exp, scale=1.0 / 16.0),
                  reads=[Lr], writes=[enGr])
            sc.op("dve", lambda e, eG=eG, h=h: e.tensor_copy(
                self.eGl[:, h, :], eG.rearrange("p (c j) -> p c j", j=128)[:, :, 127]),
                reads=[eGr], writes=[self.eGl_res[h]])
            if otts:
                wq, wqr = self.wload(self.W["glaqk"][h], [128, KT, 128])
            wk, wkr = self.wload(self.W["glaqk"][4 + h], [128, KT, 128])
            for tt in range(NTT):
                ts = slice(tt * TT, (tt + 1) * TT)
                if tt in otts:
                    ps, pres = self.mm_bank()
                    self.mm_group(ps, pres, [(wq[:, k, :], self.hT[:, k, ts]) for k in range(KT)],
                                  reads=[wqr] + hres_all(tt))
                    sc.op("dve", lambda e, h=h, ts=ts, ps=ps, eG=eG: e.scalar_tensor_tensor(
                        qk[:, h, ts], ps, 128.0 ** -0.5, eG[:, ts], ALU.mult, ALU.mult),
                        reads=[pres, eGr], writes=[qk_res[h][tt]])
                ps, pres = self.mm_bank()
                self.mm_group(ps, pres, [(wk[:, k, :], self.hT[:, k, ts]) for k in range(KT)],
                              reads=[wkr] + hres_all(tt))
                sc.op("dve", lambda e, h=h, ts=ts, ps=ps, enG=enG: e.tensor_tensor(
                    qk[:, 4 + h, ts], ps, enG[:, ts], ALU.mult),
                    reads=[pres, enGr], writes=[qk_res[4 + h][tt]])
        if stop <= 3:
            return
        for h in range(4):
            wv, wr = self.wload(self.W["glav"][:, :, h * 256:(h + 1) * 256], [128, KT, 256])
            for j in range(8):
                tcols = slice(j * 128, (j + 1) * 128)
                ps, pres = self.mm_bank()
                self.mm_group(ps[:, 0:256], pres, [(self.hT[:, k, tcols], wv[:, k, :]) for k in range(KT)],
                              reads=[wr] + hres_all(j // 4))
                eng = "act" if (j % 2 == 0) else "dve"
                if eng == "act":
                    sc.op("act", lambda e, j=j, h=h, ps=ps: e.activation(
                        vtok[:, j, h * 256:(h + 1) * 256], ps[:, 0:256], AF.Copy),
                        reads=[pres], writes=[v_res[j][h]])
                else:
                    sc.op("dve", lambda e, j=j, h=h, ps=ps: e.tensor_copy(
                        vtok[:, j, h * 256:(h + 1) * 256], ps[:, 0:256]),
                        reads=[pres], writes=[v_res[j][h]])
        if stop <= 4:
            return
        sc.barrier()
        self.reset_tmps(0, 16384)
        bank = lambda i: self.psall[:, i * 512:(i + 1) * 512]
        bA, bT, bO, bU = 4, 5, (6, 7), (0, 1)
        tbank = bank(bT).bitcast(BF16)
        stageA_all = []
        for j in range(8):
            tcols = slice(j * 128, (j + 1) * 128)
            tt = j // 4
            bAj = (bA, 2)[j % 2]
            bTj = (bT, 3)[j % 2]
            tbank = bank(bTj).bitcast(BF16)
            for h in range(4 if tt in otts else 0):
                self.mm_group(bank(bAj)[:, h * 128:(h + 1) * 128], self.bank_res[bAj],
                              [(qk[:, 4 + h, tcols], qk[:, h, tcols])],
                              reads=[qk_res[4 + h][tt], qk_res[h][tt]])
            for h in range(4):
                tp = tbank[:, h * 128:(h + 1) * 128]
                sc.op("pe", lambda e, tp=tp, h=h, tcols=tcols: e.transpose(tp, qk[:, 4 + h, tcols], self.ident),
                      reads=[qk_res[4 + h][tt], self.ident_res], writes=[self.bank_res[bTj]])
            stageA = []
            for h in range(4):
                asb, asr = self.tmp("asb", [128, 128], BF16, 32)
                aps = bank(bAj)[:, h * 128:(h + 1) * 128]
                if tt in otts:
                    sc.op("dve", lambda e, asb=asb, aps=aps: e.tensor_tensor(asb, aps, self.maskT, ALU.mult),
                          reads=[self.bank_res[bAj], self.mask_res], writes=[asr])
                ktok, ktr = self.tmp("ktok", [128, 128], BF16, 32)
                tp = tbank[:, h * 128:(h + 1) * 128]
                sc.op("act", lambda e, ktok=ktok, tp=tp: e.activation(ktok, tp, AF.Copy),
                      reads=[self.bank_res[bTj]], writes=[ktr])
                stageA.append((asb, asr, ktok, ktr))
            stageA_all.append(stageA)
        for j in range(8):
            tcols = slice(j * 128, (j + 1) * 128)
            tt = j // 4
            stageA = stageA_all[j]
            for h in range(4):
                asb, asr, ktok, ktr = stageA[h]
                ob = bO[h // 2]
                for e2 in range(2 if tt in otts else 0):
                    sl = (h % 2) * 2 + e2
                    ops = bank(ob)[:, sl * 128:(sl + 1) * 128]
                    ecols = slice(h * 256 + e2 * 128, h * 256 + (e2 + 1) * 128)
                    self.mm_group(ops, self.bank_res[ob],
                                  [(vtok[:, j, ecols], asb),
                                   (self.Sbf[:, h, e2 * 128:(e2 + 1) * 128], qk[:, h, tcols])],
                                  reads=[v_res[j][h], asr, self.Sbf_res[h], qk_res[h][tt]])
                ub = bU[h % 2]
                ups = bank(ub)[:, 0:256]
                self.mm_group(ups, self.bank_res[ub], [(ktok, vtok[:, j, h * 256:(h + 1) * 256])],
                              reads=[ktr, v_res[j][h]])
                dec = self.eGl[:, h, j:j + 1]
                Sh = self.S[:, h, :]
                sc.op("dve", lambda e, Sh=Sh, dec=dec: e.tensor_scalar(Sh, Sh, dec, None, ALU.mult),
                      reads=[self.S_res[h], self.eGl_res[h]], writes=[self.S_res[h]])
                sc.op("dve", lambda e, Sh=Sh, dec=dec, ups=ups: e.scalar_tensor_tensor(
                    Sh, ups, dec, Sh, ALU.mult, ALU.add),
                    reads=[self.bank_res[ub], self.S_res[h], self.eGl_res[h]], writes=[self.S_res[h]])
                sc.op("act", lambda e, Sh=Sh, h=h: e.activation(self.Sbf[:, h, :], Sh, AF.Copy),
                      reads=[self.S_res[h]], writes=[self.Sbf_res[h]])
            for h in range(4 if tt in otts else 0):
                ob = bO[h // 2]
                for e2 in range(2):
                    sl = (h % 2) * 2 + e2
                    ops = bank(ob)[:, sl * 128:(sl + 1) * 128]
                    f8 = 2 * h + e2
                    if e2 == 0:
                        sc.op("act", lambda e, f8=f8, tcols=tcols, ops=ops: e.activation(
                            oT[:, f8, tcols], ops, AF.Copy), reads=[self.bank_res[ob]], writes=[o_res[f8][j]])
                    else:
                        sc.op("dve", lambda e, f8=f8, tcols=tcols, ops=ops: e.tensor_copy(
                            oT[:, f8, tcols], ops), reads=[self.bank_res[ob]], writes=[o_res[f8][j]])
        sc.barrier()
        if stop <= 5 or state_only:
            return
        self.reset_tmps(T0, self.arena_bytes)
        for h in range(4):
            for tt in otts:
                ts = slice(tt * TT, (tt + 1) * TT)
                sqs, sres = [], []
                for e2 in range(2):
                    f8 = 2 * h + e2
                    sq, sqr = self.ntmp("osq", [128, TT], BF16, 4)
                    sc.op("act", lambda e, sq=sq, f8=f8, ts=ts: e.activation(sq, oT[:, f8, ts], AF.Square),
                          reads=[o_res[f8][jj] for jj in range(tt * 4, tt * 4 + 4)], writes=[sqr])
                    sqs.append(sq)
                    sres.append(sqr)
                rstd, rres = self.rstd_from_sq(sqs, sres)
                for e2 in range(2):
                    f8 = 2 * h + e2
                    o = oT[:, f8, ts]
                    ores = [o_res[f8][jj] for jj in range(tt * 4, tt * 4 + 4)]
                    sc.op("dve", lambda e, o=o, rstd=rstd: e.tensor_tensor(o, o, rstd, ALU.mult),
                          reads=ores + [rres], writes=ores)
                    sc.op("dve", lambda e, o=o, f8=f8, ts=ts, e2=e2: e.scalar_tensor_tensor(
                        sr[:, f8, ts], o, self.col("glag", e2), sr[:, f8, ts], ALU.mult, ALU.mult),
                        reads=ores + [sr_res[f8][tt]], writes=[sr_res[f8][tt]])
        sc.barrier()
        self.reset_tmps(T0, self.arena_bytes)
        self.outproj(self.W["glaout"], KT, lambda k, ts: sr[:, k, ts], lambda k, tt: sr_res[k][tt], gg="gg_m1", fused=True, tts=otts, after_tt=after_tt)

    def prologue(self):
        sc = self.sc
        nc = self.nc
        cres = Res("consts")
        self.cres = cres
        sc.dma("sp", lambda e: e.dma_start(out=self.small, in_=self.din["small"]), writes=[cres], key="small")
        self.mask_res = Res("masks")
        sc.dma("sp", lambda e: e.dma_start(out=self.maskT, in_=self.din["masks"][:, 0:128]),
               writes=[self.mask_res], key="masks")
        sc.dma("sp", lambda e: e.dma_start(out=self.rmask, in_=self.din["masks"][:, 128:128 + SEG]),
               writes=[self.mask_res], key="masks")
        idf, idr = self.arena.view(0, [128, 128], F32), Res("idf")
        sc.dma("sp", lambda e: e.dma_start(out=idf, in_=self.din["masks"][:, 128 + SEG:256 + SEG]),
               writes=[idr], key="idf")
        self.ident_res = Res("ident")
        sc.op("dve", lambda e: e.tensor_copy(self.ident, idf), reads=[idr], writes=[self.ident_res])
        waf, war = self.arena.view(1024, [16, 512], F32), Res("waf")
        sc.dma("sp", lambda e: e.dma_start(out=waf, in_=self.din["walpha"]), writes=[war], key="waf")
        self.wal_res = Res("wal")
        sc.op("dve", lambda e: e.tensor_copy(self.wal, waf), reads=[war], writes=[self.wal_res])
        epsc = self.newc("eps", 1)
        onec = self.newc("one", 1)
        sc.op("dve", lambda e: e.memset(epsc, EPS), writes=[cres])
        sc.op("dve", lambda e: e.memset(onec, 1.0), writes=[cres])
        ones_res = Res("ones")
        sc.op("dve", lambda e: e.memset(self.ones, 1.0), writes=[ones_res])
        for ap, res in ((self.fhalo_flat, self.fhalo_all), (self.rhalo_flat, self.rhalo_all),
                        (self.rstate, self.rstate_all), (self.S_flat, self.S_all), (self.Sbf_flat, self.Sbf_all)):
            sc.op("dve", lambda e, ap=ap: e.memset(ap, 0.0), writes=res)
        cact = self.cact
        cact_res = Res("cact")
        self.cact_res = cact_res
        sc.op("act", lambda e: e.activation(cact, self.col("c"), AF.Silu), reads=[cres], writes=[cact_res])
        t8 = self.newc("t8", 8)
        sc.op("act", lambda e: e.activation(t8, self.col("rglam"), AF.Exp, scale=-1.0), reads=[cres], writes=[cres])
        sc.op("act", lambda e: e.activation(t8, t8, AF.Ln, bias=1.0), reads=[cres], writes=[cres])
        clam = self.newc("clam", 8)
        chalf = self.newc("chalf", 8)
        hba = self.newc("hba", 8)
        hbx = self.newc("hbx", 8)
        ngl = self.newc("nglab", 4)
        sc.op("dve", lambda e: e.tensor_scalar(clam, t8, -8.0, None, ALU.mult), reads=[cres], writes=[cres])
        sc.op("dve", lambda e: e.tensor_scalar(chalf, t8, -4.0, None, ALU.mult), reads=[cres], writes=[cres])
        sc.op("dve", lambda e: e.tensor_scalar(hba, self.col("rgba"), 0.5, None, ALU.mult), reads=[cres], writes=[cres])
        sc.op("dve", lambda e: e.tensor_scalar(hbx, self.col("rgbx"), 0.5, None, ALU.mult), reads=[cres], writes=[cres])
        sc.op("dve", lambda e: e.tensor_scalar(ngl, self.col("glab"), -1.0, None, ALU.mult), reads=[cres], writes=[cres])
        sc.barrier()

    def mods_start(self, li, upfront=0):
        self.mm_excl = {7}
        mod = self.newc("mod%d" % li, 48)
        o = self.cc_map["mod%d" % li][0]
        self.cc_map["sh_m%d" % li] = (o, 8)
        self.cc_map["sh_f%d" % li] = (o + 24, 8)
        self.modstate = dict(li=li, items=list(range(16)), mod=mod,
                             gm=self.newc("gmod_m%d" % li, 8), ggm=self.newc("gg_m%d" % li, 8),
                             gf=self.newc("gmod_f%d" % li, 8), ggf=self.newc("gg_f%d" % li, 8))
        if upfront:
            self.fill(upfront)

    def fill(self, n=1):
        st = getattr(self, "modstate", None)
        if st is None:
            return False
        sc = self.sc
        cres = self.cres
        li = st["li"]
        mod = st["mod"]
        ps = self.psall[:, 7 * 512:8 * 512]
        pres = self.bank_res[7]
        for _ in range(n):
            if not st["items"]:
                break
            g = st["items"].pop(0)
            wv, wr = self.wload(self.W["ada"][li, 3 * g:3 * g + 3].rearrange("c p k m -> p c k m"),
                                [128, 3, KT, 128])
            for ci in range(3):
                m = 3 * g + ci
                self.mm_group(ps[:, m:m + 1], pres,
                              [(wv[:, ci, k, :], self.cact[:, k:k + 1]) for k in range(KT)],
                              reads=[wr, self.cact_res])
            if g == 5:
                sc.op("dve", lambda e, mod=mod, ps=ps, li=li: e.tensor_tensor(
                    mod[:, 0:18], ps[:, 0:18], self.col("adab%d" % li)[:, 0:18], ALU.add),
                    reads=[pres, cres], writes=[cres])
                gm = st["gm"]
                sc.op("dve", lambda e, gm=gm, mod=mod, li=li: e.scalar_tensor_tensor(
                    gm, mod[:, 8:16], 1.0, self.col("ng%d_0" % li), ALU.add, ALU.mult),
                    reads=[cres], writes=[cres])
        if not st["items"]:
            sc.op("dve", lambda e, mod=mod, ps=ps, li=li: e.tensor_tensor(
                mod[:, 18:48], ps[:, 18:48], self.col("adab%d" % li)[:, 18:48], ALU.add),
                reads=[pres, cres], writes=[cres])
            ggm, gf, ggf = st["ggm"], st["gf"], st["ggf"]
            sc.op("dve", lambda e, ggm=ggm, mod=mod, li=li: e.tensor_tensor(
                ggm, mod[:, 16:24], self.col("ng%d_1" % li), ALU.mult), reads=[cres], writes=[cres])
            sc.op("dve", lambda e, gf=gf, mod=mod, li=li: e.scalar_tensor_tensor(
                gf, mod[:, 32:40], 1.0, self.col("ng%d_2" % li), ALU.add, ALU.mult), reads=[cres], writes=[cres])
            sc.op("dve", lambda e, ggf=ggf, mod=mod, li=li: e.tensor_tensor(
                ggf, mod[:, 40:48], self.col("ng%d_3" % li), ALU.mult), reads=[cres], writes=[cres])
            self.modstate = None
            self.mm_excl = set()
        return True

    def flush(self):
        if getattr(self, "modstate", None) is not None:
            self.fill(100)
            self.sc.barrier()

    def build(self):
        nc = self.nc
        sc = self.sc
        self.din = {}
        self.din["xT"] = nc.dram_tensor("xT", [D, (self.npre + self.nseg) * SEG], F32, kind="ExternalInput").ap()
        self.din["small"] = nc.dram_tensor("small", [128, NSMALL], F32, kind="ExternalInput").ap()
        self.din["masks"] = nc.dram_tensor("masks", [128, 256 + SEG], F32, kind="ExternalInput").ap()
        self.din["walpha"] = nc.dram_tensor("walpha", [16, 512], F32, kind="ExternalInput").ap()
        self.W = {}
        for k, shp in W_SHAPES.items():
            if k == "walpha":
                continue
            self.W[k] = nc.dram_tensor(k, shp, F32, kind="ExternalInput").ap()
        outT = nc.dram_tensor("outT", [D, self.nseg * SEG], F32, kind="ExternalOutput").ap()
        self.NW = 6
        self.WSLOT = 3072
        self.arena_bytes = 122880
        with ExitStack() as st:
            def sb(name, shape, dt):
                return st.enter_context(nc.sbuf_tensor(name, shape, dt))
            xres_t = sb("xres", [128, KT * SEG], F32)
            arena_t = sb("arena", [128, self.arena_bytes // 4], F32)
            wring_t = sb("wring", [128, self.NW * self.WSLOT], BF16)
            small_t = sb("small_sb", [128, NSMALL], F32)
            cc_t = sb("cc", [128, 256], F32)
            ones_t = sb("ones", [128, 128], BF16)
            ident_t = sb("ident", [128, 128], BF16)
            maskT_t = sb("maskT", [128, 128], F32)
            rmask_t = sb("rmask", [128, SEG], F32)
            wal_t = sb("wal", [16, 512], BF16)
            cact_t = sb("cact", [128, 8], BF16)
            fhalo_t = sb("fhalo", [128, 2 * 44 * 2], F32)
            rhalo_t = sb("rhalo", [128, 8 * 3], F32)
            rstate_t = sb("rstate", [128, 8], F32)
            S_t = sb("Sst", [128, 4 * 256], F32)
            Sbf_t = sb("Sbf", [128, 4 * 256], BF16)
            eGl_t = sb("eGl", [128, 4 * 8], F32)
            ps_t = st.enter_context(nc.psum_tensor("psall", [128, 8 * 512], F32))

            self.xres = xres_t[:].rearrange("p (k t) -> p k t", k=KT)
            self.x_res = [[Res("x%d_%d" % (k, tt)) for tt in range(NTT)] for k in range(KT)]
            self.h_res = [[Res("h%d_%d" % (k, tt)) for tt in range(NTT)] for k in range(KT)]
            self.y_res = [[Res("y%d_%d" % (k, tt)) for tt in range(NTT)] for k in range(KT)]
            self.ysq_res = self.h_res
            self.arena = Arena(arena_t, self.arena_bytes // 4)
            self.wring = wring_t[:].rearrange("p (n w) -> p n w", n=self.NW)
            self.w_res = [Res("w%d" % i) for i in range(self.NW)]
            self.w_i = 0
            self.small = small_t[:]
            self.cc = cc_t[:]
            self.cc_off = 0
            self.cc_n = 256
            self.cc_map = {}
            self.ones = ones_t[:]
            self.ident = ident_t[:]
            self.maskT = maskT_t[:]
            self.rmask = rmask_t[:]
            self.wal = wal_t[:]
            self.cact = cact_t[:]
            self.fhalo_flat = fhalo_t[:]
            self.fhalo = fhalo_t[:].rearrange("p (l c k) -> p l c k", l=2, c=44)
            self.fhalo_res = [[Res("fh%d_%d" % (l, c)) for c in range(44)] for l in range(2)]
            self.fhalo_all = [r for l in self.fhalo_res for r in l]
            self.rhalo_flat = rhalo_t[:]
            self.rhalo = rhalo_t[:].rearrange("p (f k) -> p f k", f=8)
            self.rhalo_res = [Res("rh%d" % f) for f in range(8)]
            self.rhalo_all = self.rhalo_res
            self.rstate = rstate_t[:]
            self.rstate_res = [Res("rs%d" % f) for f in range(8)]
            self.rstate_all = self.rstate_res
            self.S_flat = S_t[:]
            self.S = S_t[:].rearrange("p (h e) -> p h e", h=4)
            self.S_res = [Res("S%d" % h) for h in range(4)]
            self.S_all = self.S_res
            self.Sbf_flat = Sbf_t[:]
            self.Sbf = Sbf_t[:].rearrange("p (h e) -> p h e", h=4)
            self.Sbf_res = [Res("Sb%d" % h) for h in range(4)]
            self.Sbf_all = self.Sbf_res
            self.eGl = eGl_t[:].rearrange("p (h c) -> p h c", h=4)
            self.eGl_res = [Res("eGl%d" % h) for h in range(4)]
            self.psall = ps_t[:]
            self.bank_res = [Res("bank%d" % i) for i in range(8)]
            self.mm_i = 0
            self.NT0 = self.arena_bytes - 22528
            self.ntmps = {}
            self.ntmp_off = self.NT0
            self.reset_tmps(65536, self.arena_bytes)

            self.prologue()
            self.mods_start(0, upfront=6)
            sc.barrier()
            allx = [r for l in self.x_res for r in l]
            preloaded = set()
            for seg in range(self.npre + self.nseg):
                pre = seg < self.npre
                last_pre = seg == self.npre - 1
                cols = slice(seg * SEG, (seg + 1) * SEG)
                def load_x(sg, tt):
                    c0 = sg * SEG + tt * TT
                    src = self.din["xT"][:, c0:c0 + TT].rearrange("(k p) t -> p k t", p=128)
                    dstx = self.xres[:, :, tt * TT:(tt + 1) * TT]
                    sc.dma("sp", lambda e, src=src, dstx=dstx: e.dma_start(out=dstx, in_=src),
                           writes=[self.x_res[k][tt] for k in range(KT)], key="xin%d" % tt)

                for tt in range(NTT):
                    if (seg, tt) not in preloaded:
                        load_x(seg, tt)
                nxt = None
                if seg + 1 < self.npre + self.nseg:
                    def nxt(tt, seg=seg, load_x=load_x):
                        load_x(seg + 1, tt)
                        preloaded.add((seg + 1, tt))
                def early_pre(li):
                    def cb(tt, li=li):
                        self.prenorm("gmod_f%d" % li, "sh_f%d" % li, tts=[tt])
                        self.pre_done = getattr(self, "pre_done", set()) | {(li, tt)}
                    return cb

                if "m0" in self.stages:
                    self.rglru(after_tt=(early_pre(0) if ("f0" in self.stages and seg > 0) else None))
                self.flush()
                if seg == 0:
                    self.mods_start(1)
                if "f0" in self.stages:
                    self.ffn(0)
                self.flush()
                if "m1" in self.stages:
                    self.gla(state_only=(pre and not last_pre), otts=([NTT - 1] if last_pre else None),
                             after_tt=(early_pre(1) if ("f1" in self.stages and (not pre or last_pre)) else None))
                if "f1" in self.stages:
                    if not pre:
                        self.ffn(1, final=True, after_tt=nxt)
                    elif last_pre:
                        self.ffn(1, halo_only=True)
                if last_pre:
                    sc.barrier()
                    fl = self.col("flag")
                    for ap, res in ((self.fhalo_flat, self.fhalo_all), (self.rhalo_flat, self.rhalo_all),
                                    (self.rstate, self.rstate_all), (self.S_flat, self.S_all),
                                    (self.Sbf_flat, self.Sbf_all)):
                        sc.op("dve", lambda e, ap=ap, fl=fl: e.tensor_scalar(ap, ap, fl, None, ALU.mult),
                              reads=list(res), writes=list(res))
                    sc.barrier()
                if not pre:
                    final_in_y = "f1" in self.stages
                    for tt in range(NTT):
                        c0 = (seg - self.npre) * SEG + tt * TT
                        dst = outT[:, c0:c0 + TT].rearrange("(k p) t -> p k t", p=128)
                        if final_in_y:
                            srcy = self.ymix[:, :, tt * TT:(tt + 1) * TT]
                            rds = [self.y_res[k][tt] for k in range(KT)]
                        else:
                            srcy = self.xres[:, :, tt * TT:(tt + 1) * TT]
                            rds = [self.x_res[k][tt] for k in range(KT)]
                        sc.dma("sp", lambda e, dst=dst, srcy=srcy: e.dma_start(out=dst, in_=srcy),
                               reads=rds, key="xout%d" % tt)
            sc.emit()
        return nc


def make_masks():
    m = np.zeros((128, 256 + SEG), np.float32)
    j = np.arange(128)[:, None]
    i = np.arange(128)[None, :]
    m[:, 0:128] = (j <= i).astype(np.float32)
    t = np.arange(SEG)[None, :]
    m[:, 128:128 + SEG] = (t % 128 != 0).astype(np.float32)
    m[:, 128 + SEG:256 + SEG] = np.eye(128, dtype=np.float32)
    return m


_CACHE = {}


def run(inputs, nseg=2, npre=2, stages=("m0", "f0", "m1", "f1"), trace=False):
    inp = {k: np.asarray(v) for k, v in inputs.items()}
    key = (nseg, npre, tuple(stages))
    if key not in _CACHE:
        _CACHE[key] = Builder(nseg, stages, npre=npre).build()
    nc = _CACHE[key]
    w = prep_weights(inp)
    masks = make_masks()
    T = nseg * SEG
    P = npre * SEG
    in_maps = []
    for core in range(NCORES):
        b, half = core // 2, core % 2
        xb = inp["x"][b]
        real = xb[half * T:(half + 1) * T]
        prefix = xb[0:P] if half == 0 else xb[half * T - P:half * T]
        xT = np.ascontiguousarray(np.concatenate([prefix, real], axis=0).T)
        m = {"xT": xT, "small": pack_small(inp, b, flag=float(half)), "masks": masks}
        m.update(w)
        in_maps.append(m)
    res = run_bass_kernel_spmd(nc, in_maps, core_ids=list(range(NCORES)), trace=trace)
    out = np.empty((NB, 2 * T, D), np.float32)
    for core in range(NCORES):
        b, half = core // 2, core % 2
        out[b, half * T:(half + 1) * T] = res.results[core]["outT"].T
    return out, res


def kernel(**inputs):
    out, _ = run(inputs)
    return out
```

```python
import os
from contextlib import ExitStack
import numpy as np
import concourse.bass as bass
import concourse.mybir as mybir
from concourse.bass_utils import run_bass_kernel_spmd

F32 = mybir.dt.float32
BF16 = mybir.dt.bfloat16
AF = mybir.ActivationFunctionType
ALU = mybir.AluOpType

D = 1024
S = 4096
NB = 4
DFF = 2816
KT = 8
FT = 22
SEG = 1024
TT = 512
NTT = SEG // TT
EPS = 1e-6
NCORES = 8


class Res:
    __slots__ = ("name", "w", "r")

    def __init__(self, name):
        self.name = name
        self.w = None
        self.r = []


class Sched:
    ENGS = ["pe", "act", "dve", "pool", "sp"]

    def __init__(self, nc):
        self.nc = nc
        self.q = {e: [] for e in self.ENGS}
        self.cnt = {e: 0 for e in self.ENGS}
        self.dcnt = {}
        self.seen = {e: {} for e in self.ENGS}
        self.floor = {e: {} for e in self.ENGS}

    def _deps(self, eng, reads, writes):
        need = {}

        def add(tok):
            if tok is None:
                return
            k, v = tok
            if k == eng and eng == "pe":
                return
            if need.get(k, 0) < v:
                need[k] = v

        for r in reads:
            add(r.w)
        for w in writes:
            add(w.w)
            for t in w.r:
                add(t)
        for k, v in self.floor[eng].items():
            if not (k == eng and eng == "pe"):
                if need.get(k, 0) < v:
                    need[k] = v
        self.floor[eng] = {}
        waits = []
        seen = self.seen[eng]
        for k, v in need.items():
            if seen.get(k, 0) < v:
                seen[k] = v
                waits.append((k, v))
        return waits

    def op(self, eng, fn, reads=(), writes=(), inc=True):
        waits = self._deps(eng, reads, writes)
        if inc:
            self.cnt[eng] += 1
            tok = (eng, self.cnt[eng])
        else:
            tok = (eng, self.cnt[eng] + 1)
        self.q[eng].append((waits, fn, (eng, 1) if inc else None))
        for r in reads:
            r.r.append(tok)
        for w in writes:
            w.w = tok
            w.r = []
        return tok

    def dma(self, qeng, fn, reads=(), writes=(), key="x"):
        waits = self._deps(qeng, reads, writes)
        k = "d:" + key
        self.dcnt[k] = self.dcnt.get(k, 0) + 16
        tok = (k, self.dcnt[k])
        self.q[qeng].append((waits, fn, (k, 16)))
        for r in reads:
            r.r.append(tok)
        for w in writes:
            w.w = tok
            w.r = []
        return tok

    def barrier(self, engs=("pe", "act", "dve"), final=False):
        snap = {e: self.cnt[e] for e in self.ENGS if self.cnt[e] > 0}
        snap.update({k: v for k, v in self.dcnt.items() if final or not k.startswith("d:w")})
        for e in engs:
            f = self.floor[e]
            for k, v in snap.items():
                if f.get(k, 0) < v:
                    f[k] = v

    def emit(self):
        nc = self.nc
        self.barrier(engs=("sp",), final=True)
        waits = self._deps("sp", (), ())
        self.q["sp"].append((waits, None, None))
        keys = list(self.ENGS) + list(self.dcnt.keys())
        with ExitStack() as st:
            sem = {}
            for i, k in enumerate(keys):
                sem[k] = st.enter_context(nc.semaphore("s%d" % i))
            block = st.enter_context(nc.Block())

            def run(ename, eobj):
                for waits, fn, inc in self.q[ename]:
                    for k, v in waits:
                        eobj.wait_ge(sem[k], v)
                    if fn is None:
                        continue
                    ins = fn(eobj)
                    if inc is not None:
                        ins.then_inc(sem[inc[0]], inc[1])

            block.tensor(lambda e: run("pe", e))
            block.scalar(lambda e: run("act", e))
            block.vector(lambda e: run("dve", e))
            block.gpsimd(lambda e: run("pool", e))
            block.sync(lambda e: run("sp", e))


def _prod(xs):
    r = 1
    for x in xs:
        r *= x
    return r


class Arena:
    def __init__(self, t_f32, nwords):
        self.t = t_f32
        self.nwords = nwords

    def view(self, off_bytes, shape, dtype):
        assert off_bytes % 4 == 0
        n = _prod(shape[1:])
        esz = 4 if dtype == F32 else 2
        nb = n * esz
        assert nb % 4 == 0
        assert off_bytes + nb <= self.nwords * 4, (off_bytes, nb, self.nwords * 4)
        base = self.t[0:shape[0], off_bytes // 4:(off_bytes + nb) // 4]
        ap = base if dtype == F32 else base.bitcast(dtype)
        if len(shape) == 3:
            ap = ap.rearrange("p (a b) -> p a b", a=shape[1])
        elif len(shape) == 4:
            ap = ap.rearrange("p (a b c) -> p a b c", a=shape[1], b=shape[2])
        return ap


def lay_lhsT(W):
    K, M = W.shape
    return np.ascontiguousarray(W.reshape(K // 128, 128, M // 128, 128).transpose(2, 1, 0, 3))


def lay_vec(v):
    return np.ascontiguousarray(v.reshape(-1, 128).T)


def small_layout():
    off = {}
    c = 0

    def add(name, n):
        nonlocal c
        off[name] = (c, n)
        c += n

    add("c", 8)
    add("flag", 1)
    for i in range(2):
        for j in range(4):
            add("ng%d_%d" % (i, j), 8)
        add("adab%d" % i, 48)
        add("fcw%d" % i, 44 * 3)
        add("fcb%d" % i, 44)
    add("rgcw", 8 * 4)
    add("rgcb", 8)
    add("rgba", 8)
    add("rgbx", 8)
    add("rglam", 8)
    add("glab", 4)
    add("glag", 2)
    return off, c


SM_OFF, NSMALL = small_layout()


def pack_small(inp, b, flag=1.0):
    sm = np.zeros((128, NSMALL), np.float32)

    def put(name, arr):
        o, n = SM_OFF[name]
        arr = np.asarray(arr, np.float32).reshape(128, n)
        sm[:, o:o + n] = arr

    put("c", lay_vec(inp["c"][b]))
    put("flag", np.full((128, 1), flag, np.float32))
    for i in range(2):
        for j in range(4):
            put("ng%d_%d" % (i, j), lay_vec(inp["norm_g"][i, j]))
        put("adab%d" % i, lay_vec(inp["ada_b"][i]))
        cw = inp["ffn_conv_w"][i]
        put("fcw%d" % i, np.stack([lay_vec(cw[k]) for k in range(3)], axis=-1))
        put("fcb%d" % i, lay_vec(inp["ffn_conv_b"][i]))
    rcw = inp["rg_conv_w"][0]
    put("rgcw", np.stack([lay_vec(rcw[k]) for k in range(4)], axis=-1))
    put("rgcb", lay_vec(inp["rg_conv_b"][0]))
    put("rgba", lay_vec(inp["rg_ba"][0]))
    put("rgbx", lay_vec(inp["rg_bx"][0]))
    put("rglam", lay_vec(inp["rg_lambda"][0]))
    put("glab", lay_vec(inp["gla_b_alpha"][0]))
    put("glag", lay_vec(inp["gla_norm_g"][0]))
    return sm


def prep_weights(inp):
    w = {}
    w["ada"] = np.stack([lay_lhsT(inp["ada_w"][i]) for i in range(2)])
    w["wup"] = np.stack([lay_lhsT(inp["ffn_w_up"][i]) for i in range(2)])
    w["wdn"] = np.stack([lay_lhsT(inp["ffn_w_down"][i]) for i in range(2)])
    w["rgin"] = lay_lhsT(inp["rg_w_in"][0])
    wa = inp["rg_wa"][0]
    wx = inp["rg_wx"][0]
    w["rgwa"] = np.concatenate([lay_lhsT(wa[g]) for g in range(4)], axis=0)
    w["rgwx"] = np.concatenate([lay_lhsT(wx[g]) for g in range(4)], axis=0)
    w["rgout"] = lay_lhsT(inp["rg_w_out"][0])
    gw = inp["gla_w_in"][0]
    w["glaqk"] = lay_lhsT(gw[:, 0:1024])
    w["glar"] = lay_lhsT(gw[:, 2048:3072])
    w["glav"] = np.ascontiguousarray(gw[:, 1024:2048].reshape(8, 128, 1024).transpose(1, 0, 2))
    w["glaz"] = np.ascontiguousarray(gw[:, 3072:3088].reshape(8, 128, 16).transpose(1, 0, 2))
    w["walpha"] = np.ascontiguousarray(inp["gla_w_alpha"][0])
    w["glaout"] = lay_lhsT(inp["gla_w_out"][0])
    return w


W_SHAPES = {
    "ada": [2, 48, 128, 8, 128], "wup": [2, 44, 128, 8, 128], "wdn": [2, 8, 128, 22, 128],
    "rgin": [16, 128, 8, 128], "rgwa": [8, 128, 2, 128], "rgwx": [8, 128, 2, 128],
    "rgout": [8, 128, 8, 128], "glaqk": [8, 128, 8, 128], "glar": [8, 128, 8, 128],
    "glav": [128, 8, 1024], "glaz": [128, 8, 16], "walpha": [16, 512], "glaout": [8, 128, 8, 128],
}


class Builder:
    def __init__(self, nseg, stages, npre=0):
        self.nseg = nseg
        self.npre = npre
        self.stages = stages
        self.nc = bass.Bass("TRN2", target_bir_lowering=False)
        self.POOL_M = (3, 7)
        self.sc = Sched(self.nc)

    def col(self, name, j=None, n=None):
        o, w = SM_OFF[name]
        if j is None:
            return self.small[:, o:o + w]
        return self.small[:, o + j:o + j + (1 if n is None else n)]

    def newc(self, name, n):
        o = self.cc_off
        self.cc_off += n
        assert self.cc_off <= self.cc_n
        self.cc_map[name] = (o, n)
        return self.cc[:, o:o + n]

    def cst(self, name, j=None):
        o, n = self.cc_map[name]
        if j is None:
            return self.cc[:, o:o + n]
        return self.cc[:, o + j:o + j + 1]

    def mm_bank(self):
        i = self.mm_i % 8
        self.mm_i += 1
        while i in getattr(self, "mm_excl", ()):
            i = self.mm_i % 8
            self.mm_i += 1
        return self.psall[:, i * 512:(i + 1) * 512], self.bank_res[i]

    def wload(self, src, shape):
        i = self.w_i % self.NW
        self.w_i += 1
        n = _prod(shape[1:])
        assert n <= self.WSLOT
        v = self.wring[0:shape[0], i, 0:n]
        if len(shape) == 3:
            v = v.rearrange("p (a b) -> p a b", a=shape[1])
        elif len(shape) == 4:
            v = v.rearrange("p (a b c) -> p a b c", a=shape[1], b=shape[2])
        res = self.w_res[i]
        self.sc.dma("pool", lambda e, v=v, src=src: e.dma_start(out=v, in_=src),
                    reads=(), writes=[res], key="w%d" % i)
        return v, res

    def mm_group(self, out_ap, out_res, pairs, reads):
        n = len(pairs)
        for i, (l, r) in enumerate(pairs):
            self.sc.op("pe",
                       lambda e, l=l, r=r, i=i: e.matmul(out_ap, l, r, start=(i == 0), stop=(i == n - 1)),
                       reads=reads if i == 0 else (), writes=[out_res] if i == 0 else (),
                       inc=(i == n - 1))

    def tmp(self, name, shape, dtype, nbuf):
        key = name
        if key not in self.tmps:
            n = _prod(shape[1:]) * (4 if dtype == F32 else 2)
            n = (n + 31) // 32 * 32
            bufs = []
            for b in range(nbuf):
                v = self.arena.view(self.tmp_off, shape, dtype)
                self.tmp_off += n
                assert self.tmp_off <= self.tmp_end, (name, self.tmp_off, self.tmp_end)
                bufs.append((v, Res("%s%d" % (name, b))))
            self.tmps[key] = [bufs, 0]
        ent = self.tmps[key]
        b = ent[0][ent[1] % len(ent[0])]
        ent[1] += 1
        return b

    def ntmp(self, name, shape, dtype, nbuf):
        if name not in self.ntmps:
            n = _prod(shape[1:]) * (4 if dtype == F32 else 2)
            n = (n + 31) // 32 * 32
            bufs = []
            for b in range(nbuf):
                v = self.arena.view(self.ntmp_off, shape, dtype)
                self.ntmp_off += n
                assert self.ntmp_off <= self.arena_bytes, (name, self.ntmp_off)
                bufs.append((v, Res("n%s%d" % (name, b))))
            self.ntmps[name] = [bufs, 0]
        ent = self.ntmps[name]
        b = ent[0][ent[1] % len(ent[0])]
        ent[1] += 1
        return b

    def reset_tmps(self, start, end):
        end = min(end, self.NT0)
        self.tmps = {}
        self.tmp_off = start
        self.tmp_end = end

    def rstd_from_sq(self, sq_aps, sq_res):
        ps, pres = self.mm_bank()
        self.mm_group(ps, pres, [(self.ones, a) for a in sq_aps], reads=list(sq_res))
        n = len(sq_aps) * 128
        rstd, rres = self.ntmp("rstd", [128, TT], F32, 2)
        self.sc.op("act", lambda e: e.activation(rstd, ps, AF.Ln, bias=self.cst("eps"), scale=1.0 / n),
                   reads=[pres], writes=[rres])
        self.sc.op("act", lambda e: e.activation(rstd, rstd, AF.Exp, scale=-0.5),
                   reads=[rres], writes=[rres])
        return rstd, rres

    def prenorm(self, gmod, shift, tts=None):
        sc = self.sc
        for tt in (range(NTT) if tts is None else tts):
            ts = slice(tt * TT, (tt + 1) * TT)
            sqs, sres = [], []
            for k in range(KT):
                sq, sqr = self.ntmp("sq", [128, TT], BF16, 8)
                x = self.xres[:, k, ts]
                sc.op("act", lambda e, sq=sq, x=x: e.activation(sq, x, AF.Square),
                      reads=[self.x_res[k][tt]], writes=[sqr])
                sqs.append(sq)
                sres.append(sqr)
            rstd, rres = self.rstd_from_sq(sqs, sres)
            for k in range(KT):
                t, tr = self.ntmp("nt", [128, TT], F32, 3)
                x = self.xres[:, k, ts]
                sc.op("dve", lambda e, t=t, x=x, rstd=rstd: e.tensor_tensor(t, x, rstd, ALU.mult),
                      reads=[self.x_res[k][tt], rres], writes=[tr])
                h = self.hT[:, k, ts]
                sc.op("act", lambda e, h=h, t=t, k=k: e.activation(
                    h, t, AF.Identity, bias=self.cst(shift, k), scale=self.cst(gmod, k)),
                    reads=[tr], writes=[self.h_res[k][tt]])

    def postnorm_residual(self, gg, tts=None, to_y=False, after_tt=None):
        sc = self.sc
        for tt in (range(NTT) if tts is None else tts):
            ts = slice(tt * TT, (tt + 1) * TT)
            rstd, rres = self.rstd_from_sq([self.ysq[:, m, ts] for m in range(KT)],
                                           [self.ysq_res[m][tt] for m in range(KT)])
            for m in range(KT):
                y = self.ymix[:, m, ts]
                x = self.xres[:, m, ts]
                eng = "pool" if m in self.POOL_M else "dve"
                sc.op(eng, lambda e, y=y, rstd=rstd: e.tensor_tensor(y, y, rstd, ALU.mult),
                      reads=[self.y_res[m][tt], rres], writes=[self.y_res[m][tt]])
                if to_y:
                    sc.op(eng, lambda e, y=y, x=x: e.tensor_tensor(y, x, y, ALU.add),
                          reads=[self.y_res[m][tt], self.x_res[m][tt]], writes=[self.y_res[m][tt]])
                else:
                    sc.op(eng, lambda e, y=y, x=x: e.tensor_tensor(x, x, y, ALU.add),
                          reads=[self.y_res[m][tt], self.x_res[m][tt]], writes=[self.x_res[m][tt]])
            if after_tt is not None:
                after_tt(tt)

    def outproj(self, wsrc, nk, act_ap, act_res, gg, fused=False, tts=None, to_y=False, after_tt=None):
        sc = self.sc

        def group(wv_m, wr, m, tt):
            ts = slice(tt * TT, (tt + 1) * TT)
            ps, pres = self.mm_bank()
            self.mm_group(ps, pres, [(wv_m[:, k, :], act_ap(k, ts)) for k in range(nk)],
                          reads=[wr] + [act_res(k, tt) for k in range(nk)])
            y = self.ymix[:, m, ts]
            q = self.ysq[:, m, ts]
            sc.op("act", lambda e, y=y, ps=ps, m=m: e.activation(y, ps, AF.Copy, scale=self.cst(gg, m)),
                  reads=[pres], writes=[self.y_res[m][tt]])
            sc.op("act", lambda e, q=q, ps=ps: e.activation(q, ps, AF.Square),
                  reads=[pres], writes=[self.ysq_res[m][tt]])

        if fused and nk == FT:
            ws = []
            for m in range(6):
                wv, wr = self.wload(wsrc[m], [128, nk, 128])
                ws.append((wv, [wr]))
            sc.barrier(engs=("pool",))
            ex0 = self.arena.view(94208, [128, nk, 128], BF16)
            ex0_res = Res("wex0")
            ex1 = self.arena.view(self.NT0, [128, nk, 128], BF16)
            ex1_res = [b[1] for b in self.ntmps["sq"][0][0:6]]
            sc.dma("pool", lambda e: e.dma_start(out=ex0, in_=wsrc[6]), writes=[ex0_res], key="wex0")
            sc.dma("pool", lambda e: e.dma_start(out=ex1, in_=wsrc[7]), writes=ex1_res, key="wex1")
            ws.append((ex0, [ex0_res]))
            ws.append((ex1, ex1_res))
            for tt in (range(NTT) if tts is None else tts):
                ts = slice(tt * TT, (tt + 1) * TT)
                for m in range(KT):
                    wv, wrl = ws[m]
                    ps, pres = self.mm_bank()
                    self.mm_group(ps, pres, [(wv[:, k, :], act_ap(k, ts)) for k in range(nk)],
                                  reads=list(wrl) + [act_res(k, tt) for k in range(nk)])
                    y = self.ymix[:, m, ts]
                    q = self.ysq[:, m, ts]
                    sc.op("act", lambda e, y=y, ps=ps, m=m: e.activation(y, ps, AF.Copy, scale=self.cst(gg, m)),
                          reads=[pres], writes=[self.y_res[m][tt]])
                    sc.op("act", lambda e, q=q, ps=ps: e.activation(q, ps, AF.Square),
                          reads=[pres], writes=[self.ysq_res[m][tt]])
                self.postnorm_residual(gg, tts=[tt], to_y=to_y, after_tt=after_tt)
            return
        if fused:
            ws = []
            for m2 in range(KT // 2):
                wv, wr = self.wload(wsrc[2 * m2:2 * m2 + 2].rearrange("c p k m -> p c k m"), [128, 2, nk, 128])
                ws.append((wv, wr))
            for tt in (range(NTT) if tts is None else tts):
                for m in range(KT):
                    wv, wr = ws[m // 2]
                    group(wv[:, m % 2], wr, m, tt)
                self.postnorm_residual(gg, tts=[tt])
            return
        for m in range(KT):
            wv, wr = self.wload(wsrc[m], [128, nk, 128])
            for tt in range(NTT):
                group(wv, wr, m, tt)

    def ffn(self, li, halo_only=False, final=False, after_tt=None):
        sc = self.sc
        A = self.arena
        self.hT = A.view(0, [128, KT, SEG], BF16)
        self.ysq = self.hT
        aT = A.view(16384, [128, FT, SEG], BF16)
        a_res = [[Res("a%d_%d" % (f, tt)) for tt in range(NTT)] for f in range(FT)]
        self.ymix = A.view(61440, [128, KT, SEG], F32)
        self.reset_tmps(61440, self.arena_bytes)
        self.prenorm("gmod_f%d" % li, "sh_f%d" % li, tts=([NTT - 1] if halo_only else None))
        sc.barrier(engs=("act", "dve"))
        wup = self.W["wup"]
        fcw = SM_OFF["fcw%d" % li][0]
        fcb = SM_OFF["fcb%d" % li][0]
        if halo_only:
            ts = slice(SEG - 128, SEG)
            for c in range(FT):
                wv, wr = self.wload(wup[li, c:c + FT + 1:FT].rearrange("c p k m -> p c k m"), [128, 2, KT, 128])
                for gv in range(2):
                    ch = c + gv * FT
                    ps, pres = self.mm_bank()
                    self.mm_group(ps[:, 0:128], pres, [(wv[:, gv, k, :], self.hT[:, k, ts]) for k in range(KT)],
                                  reads=[wr] + [self.h_res[k][NTT - 1] for k in range(KT)])
                    halo = self.fhalo[:, li, ch, :]
                    sc.op("act", lambda e, halo=halo, ps=ps: e.activation(halo, ps[:, 126:128], AF.Copy),
                          reads=[pres], writes=[self.fhalo_res[li][ch]])
            sc.barrier()
            return
        pend = []

        def f2_tail(c, accs):
            (ag, agr), (av, avr) = accs
            sc.op("act", lambda e, ag=ag: e.activation(ag, ag, AF.Gelu_apprx_tanh),
                  reads=[agr], writes=[agr])
            sc.op("dve", lambda e, ag=ag, av=av, c=c: e.tensor_tensor(aT[:, c, :], ag, av, ALU.mult),
                  reads=[agr, avr], writes=[a_res[c][0], a_res[c][1]])

        for c in range(FT):
            self.fill(1)
            wv, wr = self.wload(wup[li, c:c + FT + 1:FT].rearrange("c p k m -> p c k m"), [128, 2, KT, 128])
            accs = []
            for gv in range(2):
                ch = c + gv * FT
                ub, ur = self.tmp("ub", [128, SEG + 2], F32, 4)
                acc, ar = self.tmp("acc", [128, SEG], F32, 5)
                halo = self.fhalo[:, li, ch, :]
                hres = self.fhalo_res[li][ch]
                sc.op("act", lambda e, ub=ub, halo=halo: e.activation(ub[:, 0:2], halo, AF.Copy),
                      reads=[hres], writes=[ur])
                for tt in range(NTT):
                    ts = slice(tt * TT, (tt + 1) * TT)
                    ps, pres = self.mm_bank()
                    self.mm_group(ps, pres, [(wv[:, gv, k, :], self.hT[:, k, ts]) for k in range(KT)],
                                  reads=[wr] + [self.h_res[k][tt] for k in range(KT)])
                    sc.op("act", lambda e, ub=ub, ps=ps, tt=tt: e.activation(
                        ub[:, 2 + tt * TT:2 + (tt + 1) * TT], ps, AF.Copy),
                        reads=[pres], writes=[ur])
                    sc.op("act", lambda e, acc=acc, ps=ps, ts=ts, ch=ch: e.activation(
                        acc[:, ts], ps, AF.Identity,
                        bias=self.small[:, fcb + ch:fcb + ch + 1],
                        scale=self.small[:, fcw + ch * 3 + 2:fcw + ch * 3 + 3]),
                        reads=[pres], writes=[ar])
                sc.op("dve", lambda e, acc=acc, ub=ub, ch=ch: e.scalar_tensor_tensor(
                    acc, ub[:, 1:1 + SEG], self.small[:, fcw + ch * 3 + 1:fcw + ch * 3 + 2], acc,
                    ALU.mult, ALU.add), reads=[ur, ar], writes=[ar])
                sc.op("dve", lambda e, acc=acc, ub=ub, ch=ch: e.scalar_tensor_tensor(
                    acc, ub[:, 0:SEG], self.small[:, fcw + ch * 3:fcw + ch * 3 + 1], acc,
                    ALU.mult, ALU.add), reads=[ur, ar], writes=[ar])
                sc.op("dve", lambda e, ub=ub, halo=halo: e.tensor_copy(halo, ub[:, SEG:SEG + 2]),
                      reads=[ur], writes=[hres])
                accs.append((acc, ar))
            pend.append((c, accs))
            if len(pend) > 1:
                f2_tail(*pend.pop(0))
        while pend:
            f2_tail(*pend.pop(0))
        sc.barrier()
        self.reset_tmps(16384, 61440)
        self.outproj(self.W["wdn"][li], FT, lambda k, ts: aT[:, k, ts], lambda k, tt: a_res[k][tt],
                     gg="gg_f%d" % li, fused=True, to_y=final, after_tt=after_tt)

    def rglru(self):
        sc = self.sc
        A = self.arena
        self.hT = A.view(0, [128, KT, SEG], BF16)
        self.ysq = self.hT
        gy = A.view(16384, [128, KT, SEG], BF16)
        gy_res = [[Res("gy%d_%d" % (f, tt)) for tt in range(NTT)] for f in range(KT)]
        self.ymix = A.view(61440, [128, KT, SEG], F32)
        self.reset_tmps(32768, self.arena_bytes)
        self.prenorm("gmod_m0", "sh_m0")
        sc.barrier(engs=("act", "dve"))
        rgin = self.W["rgin"]
        def gate(f):
            wv, wr = self.wload(rgin[f], [128, KT, 128])
            for tt in range(NTT):
                ts = slice(tt * TT, (tt + 1) * TT)
                ps, pres = self.mm_bank()
                self.mm_group(ps, pres, [(wv[:, k, :], self.hT[:, k, ts]) for k in range(KT)],
                              reads=[wr] + [self.h_res[k][tt] for k in range(KT)])
                sc.op("act", lambda e, f=f, ts=ts, ps=ps: e.activation(gy[:, f, ts], ps, AF.Gelu_apprx_tanh),
                      reads=[pres], writes=[gy_res[f][tt]])

        cw = SM_OFF["rgcw"][0]

        def stageA(j):
            xcs = []
            for f in (2 * j, 2 * j + 1):
                wv, wr = self.wload(rgin[8 + f], [128, KT, 128])
                xb, xbr = self.tmp("xb", [128, SEG + 3], F32, 4)
                xc, xcr = self.tmp("xc", [128, SEG], F32, 2)
                xcb, xcbr = self.tmp("xcb", [128, SEG], BF16, 4)
                halo = self.rhalo[:, f, :]
                hres = self.rhalo_res[f]
                sc.op("act", lambda e, xb=xb, halo=halo: e.activation(xb[:, 0:3], halo, AF.Copy),
                      reads=[hres], writes=[xbr])
                for tt in range(NTT):
                    ts = slice(tt * TT, (tt + 1) * TT)
                    ps, pres = self.mm_bank()
                    self.mm_group(ps, pres, [(wv[:, k, :], self.hT[:, k, ts]) for k in range(KT)],
                                  reads=[wr] + [self.h_res[k][tt] for k in range(KT)])
                    sc.op("act", lambda e, xb=xb, ps=ps, tt=tt: e.activation(
                        xb[:, 3 + tt * TT:3 + (tt + 1) * TT], ps, AF.Copy),
                        reads=[pres], writes=[xbr])
                sc.op("act", lambda e, xb=xb, halo=halo: e.activation(halo, xb[:, SEG:SEG + 3], AF.Copy),
                      reads=[xbr], writes=[hres])
                sc.op("dve", lambda e, xc=xc, xb=xb, f=f: e.tensor_scalar(
                    xc, xb[:, 3:3 + SEG], self.small[:, cw + f * 4 + 3:cw + f * 4 + 4], self.col("rgcb", f),
                    ALU.mult, ALU.add), reads=[xbr], writes=[xcr])
                for kk in (2, 1):
                    sc.op("dve", lambda e, xc=xc, xb=xb, f=f, kk=kk: e.scalar_tensor_tensor(
                        xc, xb[:, kk:kk + SEG], self.small[:, cw + f * 4 + kk:cw + f * 4 + kk + 1], xc,
                        ALU.mult, ALU.add), reads=[xbr, xcr], writes=[xcr])
                sc.op("dve", lambda e, xc=xc, xcb=xcb, xb=xb, f=f: e.scalar_tensor_tensor(
                    xcb, xb[:, 0:SEG], self.small[:, cw + f * 4:cw + f * 4 + 1], xc,
                    ALU.mult, ALU.add), reads=[xbr, xcr], writes=[xcbr])
                xcs.append((xcb, xcbr))
            return xcs

        def stageB(j, xcs):
            wa, war = self.wload(self.W["rgwa"][2 * j:2 * j + 2].rearrange("c p k m -> p c k m"), [128, 2, 2, 128])
            wx, wxr = self.wload(self.W["rgwx"][2 * j:2 * j + 2].rearrange("c p k m -> p c k m"), [128, 2, 2, 128])
            ths = []
            for fi in range(2):
                f = 2 * j + fi
                tha, thar = self.tmp("tha", [128, SEG], F32, 2)
                thx, thxr = self.tmp("thx", [128, SEG], F32, 2)
                for (wv, wr, th, thr, bname) in ((wa, war, tha, thar, "hba"), (wx, wxr, thx, thxr, "hbx")):
                    for tt in range(NTT):
                        ts = slice(tt * TT, (tt + 1) * TT)
                        ps, pres = self.mm_bank()
                        self.mm_group(ps, pres, [(wv[:, fi, k, :], xcs[k][0][:, ts]) for k in range(2)],
                                      reads=[wr, xcs[0][1], xcs[1][1]])
                        sc.op("act", lambda e, th=th, ts=ts, ps=ps, bname=bname, f=f: e.activation(
                            th[:, ts], ps, AF.Tanh, bias=self.cst(bname, f), scale=0.5),
                            reads=[pres], writes=[thr])
                ths.append((tha, thar, thx, thxr))
            return ths

        def stageC(j, xcs, ths):
            bufs = []
            for fi in range(2):
                f = 2 * j + fi
                tha, thar, thx, thxr = ths[fi]
                av, avr = self.tmp("av", [128, SEG], F32, 2)
                a2, a2r = self.tmp("a2", [128, SEG], F32, 2)
                sc.op("act", lambda e, av=av, tha=tha, f=f: e.activation(
                    av, tha, AF.Exp, bias=self.cst("chalf", f), scale=self.cst("chalf", f)),
                    reads=[thar], writes=[avr])
                sc.op("act", lambda e, a2=a2, tha=tha, f=f: e.activation(
                    a2, tha, AF.Exp, bias=self.cst("clam", f), scale=self.cst("clam", f)),
                    reads=[thar], writes=[a2r])
                xcb, xcbr = xcs[fi]
                sc.op("dve", lambda e, thx=thx, xcb=xcb: e.scalar_tensor_tensor(
                    thx, thx, 1.0, xcb, ALU.add, ALU.mult), reads=[thxr, xcbr], writes=[thxr])
                bufs.append((av, avr, a2, a2r))
            for fi in range(2):
                av, avr, a2, a2r = bufs[fi]
                sc.op("dve", lambda e, a2=a2: e.tensor_scalar(a2, a2, 1.0, -1.0, ALU.min, ALU.mult),
                      reads=[a2r], writes=[a2r])
                sc.op("act", lambda e, a2=a2: e.activation(a2, a2, AF.Sqrt, bias=self.cst("one"), scale=1.0),
                      reads=[a2r], writes=[a2r])
            for fi in range(2):
                f = 2 * j + fi
                tha, thar, thx, thxr = ths[fi]
                av, avr, a2, a2r = bufs[fi]
                sc.op("dve", lambda e, thx=thx, a2=a2: e.scalar_tensor_tensor(
                    thx, thx, 0.5, a2, ALU.mult, ALU.mult), reads=[thxr, a2r], writes=[thxr])
                st = self.rstate[:, f:f + 1]
                sc.op("dve", lambda e, tha=tha, av=av, thx=thx, st=st: e.tensor_tensor_scan(
                    tha, av, thx, st, ALU.mult, ALU.add),
                    reads=[avr, thxr, self.rstate_res[f]], writes=[thar])
                sc.op("dve", lambda e, tha=tha, st=st: e.tensor_copy(st, tha[:, SEG - 1:SEG]),
                      reads=[thar], writes=[self.rstate_res[f]])
                sc.op("dve", lambda e, f=f, tha=tha: e.tensor_tensor(gy[:, f, :], gy[:, f, :], tha, ALU.mult),
                      reads=[thar, gy_res[f][0], gy_res[f][1]], writes=[gy_res[f][0], gy_res[f][1]])

        xcs_all = {}
        xcs_all[0] = stageA(0)
        for j in range(4):
            self.fill(1)
            if j + 1 < 4:
                xcs_all[j + 1] = stageA(j + 1)
            self.fill(1)
            if j % 2 == 0:
                for f in range(2 * j, 2 * j + 4):
                    gate(f)
            ths = stageB(j, xcs_all[j])
            if j % 2 == 1:
                self.fill(1)
            stageC(j, xcs_all[j], ths)
        self.flush()
        sc.barrier()
        self.reset_tmps(32768, 61440)
        self.outproj(self.W["rgout"], KT, lambda k, ts: gy[:, k, ts], lambda k, tt: gy_res[k][tt], gg="gg_m0", fused=True)

    def gla(self, state_only=False, otts=None):
        sc = self.sc
        A = self.arena
        self.hT = A.view(0, [128, KT, SEG], BF16)
        self.ysq = self.hT
        qk = A.view(16384, [128, 8, SEG], BF16)
        qk_res = [[Res("qk%d_%d" % (f, tt)) for tt in range(NTT)] for f in range(8)]
        sr = A.view(32768, [128, KT, SEG], BF16)
        sr_res = [[Res("sr%d_%d" % (f, tt)) for tt in range(NTT)] for f in range(KT)]
        vtok = A.view(49152, [128, 8, D], BF16)
        v_res = [[Res("v%d_%d" % (j, h)) for h in range(4)] for j in range(8)]
        oT = A.view(65536, [128, KT, SEG], F32)
        o_res = [[Res("o%d_%d" % (f, j)) for j in range(8)] for f in range(KT)]
        self.ymix = A.view(61440, [128, KT, SEG], F32)
        T0 = 98304
        self.reset_tmps(65536, self.arena_bytes)
        self.prenorm("gmod_m1", "sh_m1")
        sc.barrier(engs=("act", "dve"))
        hres_all = lambda tt: [self.h_res[k][tt] for k in range(KT)]
        if otts is None:
            otts = list(range(NTT))
        if state_only:
            otts = []
        for f in range(0 if otts else KT, KT):
            wv, wr = self.wload(self.W["glar"][f], [128, KT, 128])
            for tt in otts:
                ts = slice(tt * TT, (tt + 1) * TT)
                ps, pres = self.mm_bank()
                self.mm_group(ps, pres, [(wv[:, k, :], self.hT[:, k, ts]) for k in range(KT)],
                              reads=[wr] + hres_all(tt))
                sc.op("act", lambda e, f=f, ts=ts, ps=ps: e.activation(sr[:, f, ts], ps, AF.Silu),
                      reads=[pres], writes=[sr_res[f][tt]])
        stop = int(os.environ.get("GLA_STOP", "9"))
        if stop <= 1:
            return
        wz, wzr = self.wload(self.W["glaz"], [128, KT, 16])
        zT, zr = self.tmp("zT", [16, SEG], BF16, 1)
        for tt in range(NTT):
            ts = slice(tt * TT, (tt + 1) * TT)
            ps, pres = self.mm_bank()
            self.mm_group(ps[0:16, :], pres, [(wz[:, k, :], self.hT[:, k, ts]) for k in range(KT)],
                          reads=[wzr] + hres_all(tt))
            sc.op("act", lambda e, ts=ts, ps=ps: e.activation(zT[:, ts], ps[0:16, :], AF.Copy),
                  reads=[pres], writes=[zr])
        if stop <= 2:
            return
        for h in range(4):
            lsp, lr = self.tmp("lsp", [128, SEG], F32, 1)
            Lc, Lr = self.tmp("Lc", [128, SEG], F32, 1)
            eG, eGr = self.tmp("eG", [128, SEG], F32, 1)
            enG, enGr = self.tmp("enG", [128, SEG], F32, 1)
            for tt in range(NTT):
                ts = slice(tt * TT, (tt + 1) * TT)
                ps, pres = self.mm_bank()
                self.mm_group(ps, pres, [(self.wal[:, h * 128:(h + 1) * 128], zT[:, ts])], reads=[zr, self.wal_res])
                sc.op("act", lambda e, lsp=lsp, ts=ts, ps=ps, h=h: e.activation(
                    lsp[:, ts], ps, AF.Exp, bias=self.cst("nglab", h), scale=-1.0),
                    reads=[pres], writes=[lr])
            sc.op("act", lambda e, lsp=lsp: e.activation(lsp, lsp, AF.Ln, bias=1.0),
                  reads=[lr], writes=[lr])
            sc.op("dve", lambda e, Lc=Lc, lsp=lsp: e.tensor_tensor_scan(
                Lc, self.rmask, lsp, 0.0, ALU.mult, ALU.add), reads=[lr, self.mask_res], writes=[Lr])
            sc.op("act", lambda e, eG=eG, Lc=Lc: e.activation(eG, Lc, AF.Exp, scale=-1.0 / 16.0),
                  reads=[Lr], writes=[eGr])
            sc.op("act", lambda e, enG=enG, Lc=Lc: e.activation(enG, Lc, AF.Exp, scale=1.0 / 16.0),
                  reads=[Lr], writes=[enGr])
            sc.op("dve", lambda e, eG=eG, h=h: e.tensor_copy(
                self.eGl[:, h, :], eG.rearrange("p (c j) -> p c j", j=128)[:, :, 127]),
                reads=[eGr], writes=[self.eGl_res[h]])
            if otts:
                wq, wqr = self.wload(self.W["glaqk"][h], [128, KT, 128])
            wk, wkr = self.wload(self.W["glaqk"][4 + h], [128, KT, 128])
            for tt in range(NTT):
                ts = slice(tt * TT, (tt + 1) * TT)
                if tt in otts:
                    ps, pres = self.mm_bank()
                    self.mm_group(ps, pres, [(wq[:, k, :], self.hT[:, k, ts]) for k in range(KT)],
                                  reads=[wqr] + hres_all(tt))
                    sc.op("dve", lambda e, h=h, ts=ts, ps=ps, eG=eG: e.scalar_tensor_tensor(
                        qk[:, h, ts], ps, 128.0 ** -0.5, eG[:, ts], ALU.mult, ALU.mult),
                        reads=[pres, eGr], writes=[qk_res[h][tt]])
                ps, pres = self.mm_bank()
                self.mm_group(ps, pres, [(wk[:, k, :], self.hT[:, k, ts]) for k in range(KT)],
                              reads=[wkr] + hres_all(tt))
                sc.op("dve", lambda e, h=h, ts=ts, ps=ps, enG=enG: e.tensor_tensor(
                    qk[:, 4 + h, ts], ps, enG[:, ts], ALU.mult),
                    reads=[pres, enGr], writes=[qk_res[4 + h][tt]])
        if stop <= 3:
            return
        for h in range(4):
            wv, wr = self.wload(self.W["glav"][:, :, h * 256:(h + 1) * 256], [128, KT, 256])
            for j in range(8):
                tcols = slice(j * 128, (j + 1) * 128)
                ps, pres = self.mm_bank()
                self.mm_group(ps[:, 0:256], pres, [(self.hT[:, k, tcols], wv[:, k, :]) for k in range(KT)],
                              reads=[wr] + hres_all(j // 4))
                eng = "act" if (j % 2 == 0) else "dve"
                if eng == "act":
                    sc.op("act", lambda e, j=j, h=h, ps=ps: e.activation(
                        vtok[:, j, h * 256:(h + 1) * 256], ps[:, 0:256], AF.Copy),
                        reads=[pres], writes=[v_res[j][h]])
                else:
                    sc.op("dve", lambda e, j=j, h=h, ps=ps: e.tensor_copy(
                        vtok[:, j, h * 256:(h + 1) * 256], ps[:, 0:256]),
                        reads=[pres], writes=[v_res[j][h]])
        if stop <= 4:
            return
        sc.barrier()
        self.reset_tmps(0, 16384)
        bank = lambda i: self.psall[:, i * 512:(i + 1) * 512]
        bA, bT, bO, bU = 4, 5, (6, 7), (0, 1)
        tbank = bank(bT).bitcast(BF16)
        stageA_all = []
        for j in range(8):
            tcols = slice(j * 128, (j + 1) * 128)
            tt = j // 4
            bAj = (bA, 2)[j % 2]
            bTj = (bT, 3)[j % 2]
            tbank = bank(bTj).bitcast(BF16)
            for h in range(4 if tt in otts else 0):
                self.mm_group(bank(bAj)[:, h * 128:(h + 1) * 128], self.bank_res[bAj],
                              [(qk[:, 4 + h, tcols], qk[:, h, tcols])],
                              reads=[qk_res[4 + h][tt], qk_res[h][tt]])
            for h in range(4):
                tp = tbank[:, h * 128:(h + 1) * 128]
                sc.op("pe", lambda e, tp=tp, h=h, tcols=tcols: e.transpose(tp, qk[:, 4 + h, tcols], self.ident),
                      reads=[qk_res[4 + h][tt], self.ident_res], writes=[self.bank_res[bTj]])
            stageA = []
            for h in range(4):
                asb, asr = self.tmp("asb", [128, 128], BF16, 32)
                aps = bank(bAj)[:, h * 128:(h + 1) * 128]
                if tt in otts:
                    sc.op("dve", lambda e, asb=asb, aps=aps: e.tensor_tensor(asb, aps, self.maskT, ALU.mult),
                          reads=[self.bank_res[bAj], self.mask_res], writes=[asr])
                ktok, ktr = self.tmp("ktok", [128, 128], BF16, 32)
                tp = tbank[:, h * 128:(h + 1) * 128]
                sc.op("act", lambda e, ktok=ktok, tp=tp: e.activation(ktok, tp, AF.Copy),
                      reads=[self.bank_res[bTj]], writes=[ktr])
                stageA.append((asb, asr, ktok, ktr))
            stageA_all.append(stageA)
        for j in range(8):
            tcols = slice(j * 128, (j + 1) * 128)
            tt = j // 4
            stageA = stageA_all[j]
            for h in range(4):
                asb, asr, ktok, ktr = stageA[h]
                ob = bO[h // 2]
                for e2 in range(2 if tt in otts else 0):
                    sl = (h % 2) * 2 + e2
                    ops = bank(ob)[:, sl * 128:(sl + 1) * 128]
                    ecols = slice(h * 256 + e2 * 128, h * 256 + (e2 + 1) * 128)
                    self.mm_group(ops, self.bank_res[ob],
                                  [(vtok[:, j, ecols], asb),
                                   (self.Sbf[:, h, e2 * 128:(e2 + 1) * 128], qk[:, h, tcols])],
                                  reads=[v_res[j][h], asr, self.Sbf_res[h], qk_res[h][tt]])
                ub = bU[h % 2]
                ups = bank(ub)[:, 0:256]
                self.mm_group(ups, self.bank_res[ub], [(ktok, vtok[:, j, h * 256:(h + 1) * 256])],
                              reads=[ktr, v_res[j][h]])
                dec = self.eGl[:, h, j:j + 1]
                Sh = self.S[:, h, :]
                sc.op("dve", lambda e, Sh=Sh, dec=dec: e.tensor_scalar(Sh, Sh, dec, None, ALU.mult),
                      reads=[self.S_res[h], self.eGl_res[h]], writes=[self.S_res[h]])
                sc.op("dve", lambda e, Sh=Sh, dec=dec, ups=ups: e.scalar_tensor_tensor(
                    Sh, ups, dec, Sh, ALU.mult, ALU.add),
                    reads=[self.bank_res[ub], self.S_res[h], self.eGl_res[h]], writes=[self.S_res[h]])
                sc.op("act", lambda e, Sh=Sh, h=h: e.activation(self.Sbf[:, h, :], Sh, AF.Copy),
                      reads=[self.S_res[h]], writes=[self.Sbf_res[h]])
            for h in range(4 if tt in otts else 0):
                ob = bO[h // 2]
                for e2 in range(2):
                    sl = (h % 2) * 2 + e2
                    ops = bank(ob)[:, sl * 128:(sl + 1) * 128]
                    f8 = 2 * h + e2
                    if e2 == 0:
                        sc.op("act", lambda e, f8=f8, tcols=tcols, ops=ops: e.activation(
                            oT[:, f8, tcols], ops, AF.Copy), reads=[self.bank_res[ob]], writes=[o_res[f8][j]])
                    else:
                        sc.op("dve", lambda e, f8=f8, tcols=tcols, ops=ops: e.tensor_copy(
                            oT[:, f8, tcols], ops), reads=[self.bank_res[ob]], writes=[o_res[f8][j]])
        sc.barrier()
        if stop <= 5 or state_only:
            return
        self.reset_tmps(T0, self.arena_bytes)
        for h in range(4):
            for tt in otts:
                ts = slice(tt * TT, (tt + 1) * TT)
                sqs, sres = [], []
                for e2 in range(2):
                    f8 = 2 * h + e2
                    sq, sqr = self.ntmp("osq", [128, TT], BF16, 4)
                    sc.op("act", lambda e, sq=sq, f8=f8, ts=ts: e.activation(sq, oT[:, f8, ts], AF.Square),
                          reads=[o_res[f8][jj] for jj in range(tt * 4, tt * 4 + 4)], writes=[sqr])
                    sqs.append(sq)
                    sres.append(sqr)
                rstd, rres = self.rstd_from_sq(sqs, sres)
                for e2 in range(2):
                    f8 = 2 * h + e2
                    o = oT[:, f8, ts]
                    ores = [o_res[f8][jj] for jj in range(tt * 4, tt * 4 + 4)]
                    sc.op("dve", lambda e, o=o, rstd=rstd: e.tensor_tensor(o, o, rstd, ALU.mult),
                          reads=ores + [rres], writes=ores)
                    sc.op("dve", lambda e, o=o, f8=f8, ts=ts, e2=e2: e.scalar_tensor_tensor(
                        sr[:, f8, ts], o, self.col("glag", e2), sr[:, f8, ts], ALU.mult, ALU.mult),
                        reads=ores + [sr_res[f8][tt]], writes=[sr_res[f8][tt]])
        sc.barrier()
        self.reset_tmps(T0, self.arena_bytes)
        self.outproj(self.W["glaout"], KT, lambda k, ts: sr[:, k, ts], lambda k, tt: sr_res[k][tt], gg="gg_m1", fused=True, tts=otts)

    def prologue(self):
        sc = self.sc
        nc = self.nc
        cres = Res("consts")
        self.cres = cres
        sc.dma("sp", lambda e: e.dma_start(out=self.small, in_=self.din["small"]), writes=[cres], key="small")
        self.mask_res = Res("masks")
        sc.dma("sp", lambda e: e.dma_start(out=self.maskT, in_=self.din["masks"][:, 0:128]),
               writes=[self.mask_res], key="masks")
        sc.dma("sp", lambda e: e.dma_start(out=self.rmask, in_=self.din["masks"][:, 128:128 + SEG]),
               writes=[self.mask_res], key="masks")
        idf, idr = self.arena.view(0, [128, 128], F32), Res("idf")
        sc.dma("sp", lambda e: e.dma_start(out=idf, in_=self.din["masks"][:, 128 + SEG:256 + SEG]),
               writes=[idr], key="idf")
        self.ident_res = Res("ident")
        sc.op("dve", lambda e: e.tensor_copy(self.ident, idf), reads=[idr], writes=[self.ident_res])
        waf, war = self.arena.view(1024, [16, 512], F32), Res("waf")
        sc.dma("sp", lambda e: e.dma_start(out=waf, in_=self.din["walpha"]), writes=[war], key="waf")
        self.wal_res = Res("wal")
        sc.op("dve", lambda e: e.tensor_copy(self.wal, waf), reads=[war], writes=[self.wal_res])
        epsc = self.newc("eps", 1)
        onec = self.newc("one", 1)
        sc.op("dve", lambda e: e.memset(epsc, EPS), writes=[cres])
        sc.op("dve", lambda e: e.memset(onec, 1.0), writes=[cres])
        ones_res = Res("ones")
        sc.op("dve", lambda e: e.memset(self.ones, 1.0), writes=[ones_res])
        for ap, res in ((self.fhalo_flat, self.fhalo_all), (self.rhalo_flat, self.rhalo_all),
                        (self.rstate, self.rstate_all), (self.S_flat, self.S_all), (self.Sbf_flat, self.Sbf_all)):
            sc.op("dve", lambda e, ap=ap: e.memset(ap, 0.0), writes=res)
        cact = self.cact
        cact_res = Res("cact")
        self.cact_res = cact_res
        sc.op("act", lambda e: e.activation(cact, self.col("c"), AF.Silu), reads=[cres], writes=[cact_res])
        t8 = self.newc("t8", 8)
        sc.op("act", lambda e: e.activation(t8, self.col("rglam"), AF.Exp, scale=-1.0), reads=[cres], writes=[cres])
        sc.op("act", lambda e: e.activation(t8, t8, AF.Ln, bias=1.0), reads=[cres], writes=[cres])
        clam = self.newc("clam", 8)
        chalf = self.newc("chalf", 8)
        hba = self.newc("hba", 8)
        hbx = self.newc("hbx", 8)
        ngl = self.newc("nglab", 4)
        sc.op("dve", lambda e: e.tensor_scalar(clam, t8, -8.0, None, ALU.mult), reads=[cres], writes=[cres])
        sc.op("dve", lambda e: e.tensor_scalar(chalf, t8, -4.0, None, ALU.mult), reads=[cres], writes=[cres])
        sc.op("dve", lambda e: e.tensor_scalar(hba, self.col("rgba"), 0.5, None, ALU.mult), reads=[cres], writes=[cres])
        sc.op("dve", lambda e: e.tensor_scalar(hbx, self.col("rgbx"), 0.5, None, ALU.mult), reads=[cres], writes=[cres])
        sc.op("dve", lambda e: e.tensor_scalar(ngl, self.col("glab"), -1.0, None, ALU.mult), reads=[cres], writes=[cres])
        sc.barrier()

    def mods_start(self, li, upfront=0):
        self.mm_excl = {7}
        mod = self.newc("mod%d" % li, 48)
        o = self.cc_map["mod%d" % li][0]
        self.cc_map["sh_m%d" % li] = (o, 8)
        self.cc_map["sh_f%d" % li] = (o + 24, 8)
        self.modstate = dict(li=li, items=list(range(16)), mod=mod,
                             gm=self.newc("gmod_m%d" % li, 8), ggm=self.newc("gg_m%d" % li, 8),
                             gf=self.newc("gmod_f%d" % li, 8), ggf=self.newc("gg_f%d" % li, 8))
        if upfront:
            self.fill(upfront)

    def fill(self, n=1):
        st = getattr(self, "modstate", None)
        if st is None:
            return False
        sc = self.sc
        cres = self.cres
        li = st["li"]
        mod = st["mod"]
        ps = self.psall[:, 7 * 512:8 * 512]
        pres = self.bank_res[7]
        for _ in range(n):
            if not st["items"]:
                break
            g = st["items"].pop(0)
            wv, wr = self.wload(self.W["ada"][li, 3 * g:3 * g + 3].rearrange("c p k m -> p c k m"),
                                [128, 3, KT, 128])
            for ci in range(3):
                m = 3 * g + ci
                self.mm_group(ps[:, m:m + 1], pres,
                              [(wv[:, ci, k, :], self.cact[:, k:k + 1]) for k in range(KT)],
                              reads=[wr, self.cact_res])
            if g == 5:
                sc.op("dve", lambda e, mod=mod, ps=ps, li=li: e.tensor_tensor(
                    mod[:, 0:18], ps[:, 0:18], self.col("adab%d" % li)[:, 0:18], ALU.add),
                    reads=[pres, cres], writes=[cres])
                gm = st["gm"]
                sc.op("dve", lambda e, gm=gm, mod=mod, li=li: e.scalar_tensor_tensor(
                    gm, mod[:, 8:16], 1.0, self.col("ng%d_0" % li), ALU.add, ALU.mult),
                    reads=[cres], writes=[cres])
        if not st["items"]:
            sc.op("dve", lambda e, mod=mod, ps=ps, li=li: e.tensor_tensor(
                mod[:, 18:48], ps[:, 18:48], self.col("adab%d" % li)[:, 18:48], ALU.add),
                reads=[pres, cres], writes=[cres])
            ggm, gf, ggf = st["ggm"], st["gf"], st["ggf"]
            sc.op("dve", lambda e, ggm=ggm, mod=mod, li=li: e.tensor_tensor(
                ggm, mod[:, 16:24], self.col("ng%d_1" % li), ALU.mult), reads=[cres], writes=[cres])
            sc.op("dve", lambda e, gf=gf, mod=mod, li=li: e.scalar_tensor_tensor(
                gf, mod[:, 32:40], 1.0, self.col("ng%d_2" % li), ALU.add, ALU.mult), reads=[cres], writes=[cres])
            sc.op("dve", lambda e, ggf=ggf, mod=mod, li=li: e.tensor_tensor(
                ggf, mod[:, 40:48], self.col("ng%d_3" % li), ALU.mult), reads=[cres], writes=[cres])
            self.modstate = None
            self.mm_excl = set()
        return True

    def flush(self):
        if getattr(self, "modstate", None) is not None:
            self.fill(100)
            self.sc.barrier()

    def build(self):
        nc = self.nc
        sc = self.sc
        self.din = {}
        self.din["xT"] = nc.dram_tensor("xT", [D, (self.npre + self.nseg) * SEG], F32, kind="ExternalInput").ap()
        self.din["small"] = nc.dram_tensor("small", [128, NSMALL], F32, kind="ExternalInput").ap()
        self.din["masks"] = nc.dram_tensor("masks", [128, 256 + SEG], F32, kind="ExternalInput").ap()
        self.din["walpha"] = nc.dram_tensor("walpha", [16, 512], F32, kind="ExternalInput").ap()
        self.W = {}
        for k, shp in W_SHAPES.items():
            if k == "walpha":
                continue
            self.W[k] = nc.dram_tensor(k, shp, F32, kind="ExternalInput").ap()
        outT = nc.dram_tensor("outT", [D, self.nseg * SEG], F32, kind="ExternalOutput").ap()
        self.NW = 6
        self.WSLOT = 3072
        self.arena_bytes = 122880
        with ExitStack() as st:
            def sb(name, shape, dt):
                return st.enter_context(nc.sbuf_tensor(name, shape, dt))
            xres_t = sb("xres", [128, KT * SEG], F32)
            arena_t = sb("arena", [128, self.arena_bytes // 4], F32)
            wring_t = sb("wring", [128, self.NW * self.WSLOT], BF16)
            small_t = sb("small_sb", [128, NSMALL], F32)
            cc_t = sb("cc", [128, 256], F32)
            ones_t = sb("ones", [128, 128], BF16)
            ident_t = sb("ident", [128, 128], BF16)
            maskT_t = sb("maskT", [128, 128], F32)
            rmask_t = sb("rmask", [128, SEG], F32)
            wal_t = sb("wal", [16, 512], BF16)
            cact_t = sb("cact", [128, 8], BF16)
            fhalo_t = sb("fhalo", [128, 2 * 44 * 2], F32)
            rhalo_t = sb("rhalo", [128, 8 * 3], F32)
            rstate_t = sb("rstate", [128, 8], F32)
            S_t = sb("Sst", [128, 4 * 256], F32)
            Sbf_t = sb("Sbf", [128, 4 * 256], BF16)
            eGl_t = sb("eGl", [128, 4 * 8], F32)
            ps_t = st.enter_context(nc.psum_tensor("psall", [128, 8 * 512], F32))

            self.xres = xres_t[:].rearrange("p (k t) -> p k t", k=KT)
            self.x_res = [[Res("x%d_%d" % (k, tt)) for tt in range(NTT)] for k in range(KT)]
            self.h_res = [[Res("h%d_%d" % (k, tt)) for tt in range(NTT)] for k in range(KT)]
            self.y_res = [[Res("y%d_%d" % (k, tt)) for tt in range(NTT)] for k in range(KT)]
            self.ysq_res = self.h_res
            self.arena = Arena(arena_t, self.arena_bytes // 4)
            self.wring = wring_t[:].rearrange("p (n w) -> p n w", n=self.NW)
            self.w_res = [Res("w%d" % i) for i in range(self.NW)]
            self.w_i = 0
            self.small = small_t[:]
            self.cc = cc_t[:]
            self.cc_off = 0
            self.cc_n = 256
            self.cc_map = {}
            self.ones = ones_t[:]
            self.ident = ident_t[:]
            self.maskT = maskT_t[:]
            self.rmask = rmask_t[:]
            self.wal = wal_t[:]
            self.cact = cact_t[:]
            self.fhalo_flat = fhalo_t[:]
            self.fhalo = fhalo_t[:].rearrange("p (l c k) -> p l c k", l=2, c=44)
            self.fhalo_res = [[Res("fh%d_%d" % (l, c)) for c in range(44)] for l in range(2)]
            self.fhalo_all = [r for l in self.fhalo_res for r in l]
            self.rhalo_flat = rhalo_t[:]
            self.rhalo = rhalo_t[:].rearrange("p (f k) -> p f k", f=8)
            self.rhalo_res = [Res("rh%d" % f) for f in range(8)]
            self.rhalo_all = self.rhalo_res
            self.rstate = rstate_t[:]
            self.rstate_res = [Res("rs%d" % f) for f in range(8)]
            self.rstate_all = self.rstate_res
            self.S_flat = S_t[:]
            self.S = S_t[:].rearrange("p (h e) -> p h e", h=4)
            self.S_res = [Res("S%d" % h) for h in range(4)]
            self.S_all = self.S_res
            self.Sbf_flat = Sbf_t[:]
            self.Sbf = Sbf_t[:].rearrange("p (h e) -> p h e", h=4)
            self.Sbf_res = [Res("Sb%d" % h) for h in range(4)]
            self.Sbf_all = self.Sbf_res
            self.eGl = eGl_t[:].rearrange("p (h c) -> p h c", h=4)
            self.eGl_res = [Res("eGl%d" % h) for h in range(4)]
            self.psall = ps_t[:]
            self.bank_res = [Res("bank%d" % i) for i in range(8)]
            self.mm_i = 0
            self.NT0 = self.arena_bytes - 22528
            self.ntmps = {}
            self.ntmp_off = self.NT0
            self.reset_tmps(65536, self.arena_bytes)

            self.prologue()
            self.mods_start(0, upfront=6)
            sc.barrier()
            allx = [r for l in self.x_res for r in l]
            preloaded = set()
            for seg in range(self.npre + self.nseg):
                pre = seg < self.npre
                last_pre = seg == self.npre - 1
                cols = slice(seg * SEG, (seg + 1) * SEG)
                def load_x(sg, tt):
                    c0 = sg * SEG + tt * TT
                    src = self.din["xT"][:, c0:c0 + TT].rearrange("(k p) t -> p k t", p=128)
                    dstx = self.xres[:, :, tt * TT:(tt + 1) * TT]
                    sc.dma("sp", lambda e, src=src, dstx=dstx: e.dma_start(out=dstx, in_=src),
                           writes=[self.x_res[k][tt] for k in range(KT)], key="xin%d" % tt)

                for tt in range(NTT):
                    if (seg, tt) not in preloaded:
                        load_x(seg, tt)
                nxt = None
                if seg + 1 < self.npre + self.nseg:
                    def nxt(tt, seg=seg, load_x=load_x):
                        load_x(seg + 1, tt)
                        preloaded.add((seg + 1, tt))
                if "m0" in self.stages:
                    self.rglru()
                self.flush()
                if seg == 0:
                    self.mods_start(1)
                if "f0" in self.stages:
                    self.ffn(0)
                self.flush()
                if "m1" in self.stages:
                    self.gla(state_only=(pre and not last_pre), otts=([NTT - 1] if last_pre else None))
                if "f1" in self.stages:
                    if not pre:
                        self.ffn(1, final=True, after_tt=nxt)
                    elif last_pre:
                        self.ffn(1, halo_only=True)
                if last_pre:
                    sc.barrier()
                    fl = self.col("flag")
                    for ap, res in ((self.fhalo_flat, self.fhalo_all), (self.rhalo_flat, self.rhalo_all),
                                    (self.rstate, self.rstate_all), (self.S_flat, self.S_all),
                                    (self.Sbf_flat, self.Sbf_all)):
                        sc.op("dve", lambda e, ap=ap, fl=fl: e.tensor_scalar(ap, ap, fl, None, ALU.mult),
                              reads=list(res), writes=list(res))
                    sc.barrier()
                if not pre:
                    final_in_y = "f1" in self.stages
                    for tt in range(NTT):
                        c0 = (seg - self.npre) * SEG + tt * TT
                        dst = outT[:, c0:c0 + TT].rearrange("(k p) t -> p k t", p=128)
                        if final_in_y:
                            srcy = self.ymix[:, :, tt * TT:(tt + 1) * TT]
                            rds = [self.y_res[k][tt] for k in range(KT)]
                        else:
                            srcy = self.xres[:, :, tt * TT:(tt + 1) * TT]
                            rds = [self.x_res[k][tt] for k in range(KT)]
                        sc.dma("sp", lambda e, dst=dst, srcy=srcy: e.dma_start(out=dst, in_=srcy),
                               reads=rds, key="xout%d" % tt)
            sc.emit()
        return nc


def make_masks():
    m = np.zeros((128, 256 + SEG), np.float32)
    j = np.arange(128)[:, None]
    i = np.arange(128)[None, :]
    m[:, 0:128] = (j <= i).astype(np.float32)
    t = np.arange(SEG)[None, :]
    m[:, 128:128 + SEG] = (t % 128 != 0).astype(np.float32)
    m[:, 128 + SEG:256 + SEG] = np.eye(128, dtype=np.float32)
    return m


_CACHE = {}


def run(inputs, nseg=2, npre=2, stages=("m0", "f0", "m1", "f1"), trace=False):
    inp = {k: np.asarray(v) for k, v in inputs.items()}
    key = (nseg, npre, tuple(stages))
    if key not in _CACHE:
        _CACHE[key] = Builder(nseg, stages, npre=npre).build()
    nc = _CACHE[key]
    w = prep_weights(inp)
    masks = make_masks()
    T = nseg * SEG
    P = npre * SEG
    in_maps = []
    for core in range(NCORES):
        b, half = core // 2, core % 2
        xb = inp["x"][b]
        real = xb[half * T:(half + 1) * T]
        prefix = xb[0:P] if half == 0 else xb[half * T - P:half * T]
        xT = np.ascontiguousarray(np.concatenate([prefix, real], axis=0).T)
        m = {"xT": xT, "small": pack_small(inp, b, flag=float(half)), "masks": masks}
        m.update(w)
        in_maps.append(m)
    res = run_bass_kernel_spmd(nc, in_maps, core_ids=list(range(NCORES)), trace=trace)
    out = np.empty((NB, 2 * T, D), np.float32)
    for core in range(NCORES):
        b, half = core // 2, core % 2
        out[b, half * T:(half + 1) * T] = res.results[core]["outT"].T
    return out, res


def kernel(**inputs):
    out, _ = run(inputs)
    return out
```

```python
import os
from contextlib import ExitStack
import numpy as np
import concourse.bass as bass
import concourse.mybir as mybir
from concourse.bass_utils import run_bass_kernel_spmd

F32 = mybir.dt.float32
BF16 = mybir.dt.bfloat16
AF = mybir.ActivationFunctionType
ALU = mybir.AluOpType

D = 1024
S = 4096
NB = 4
DFF = 2816
KT = 8
FT = 22
SEG = 1024
TT = 512
NTT = SEG // TT
EPS = 1e-6
NCORES = 8


class Res:
    __slots__ = ("name", "w", "r")

    def __init__(self, name):
        self.name = name
        self.w = None
        self.r = []


class Sched:
    ENGS = ["pe", "act", "dve", "pool", "sp"]

    def __init__(self, nc):
        self.nc = nc
        self.q = {e: [] for e in self.ENGS}
        self.cnt = {e: 0 for e in self.ENGS}
        self.dcnt = {}
        self.seen = {e: {} for e in self.ENGS}
        self.floor = {e: {} for e in self.ENGS}

    def _deps(self, eng, reads, writes):
        need = {}

        def add(tok):
            if tok is None:
                return
            k, v = tok
            if k == eng and eng == "pe":
                return
            if need.get(k, 0) < v:
                need[k] = v

        for r in reads:
            add(r.w)
        for w in writes:
            add(w.w)
            for t in w.r:
                add(t)
        for k, v in self.floor[eng].items():
            if not (k == eng and eng == "pe"):
                if need.get(k, 0) < v:
                    need[k] = v
        self.floor[eng] = {}
        waits = []
        seen = self.seen[eng]
        for k, v in need.items():
            if seen.get(k, 0) < v:
                seen[k] = v
                waits.append((k, v))
        return waits

    def op(self, eng, fn, reads=(), writes=(), inc=True):
        waits = self._deps(eng, reads, writes)
        if inc:
            self.cnt[eng] += 1
            tok = (eng, self.cnt[eng])
        else:
            tok = (eng, self.cnt[eng] + 1)
        self.q[eng].append((waits, fn, (eng, 1) if inc else None))
        for r in reads:
            r.r.append(tok)
        for w in writes:
            w.w = tok
            w.r = []
        return tok

    def dma(self, qeng, fn, reads=(), writes=(), key="x"):
        waits = self._deps(qeng, reads, writes)
        k = "d:" + key
        self.dcnt[k] = self.dcnt.get(k, 0) + 16
        tok = (k, self.dcnt[k])
        self.q[qeng].append((waits, fn, (k, 16)))
        for r in reads:
            r.r.append(tok)
        for w in writes:
            w.w = tok
            w.r = []
        return tok

    def barrier(self, engs=("pe", "act", "dve"), final=False):
        snap = {e: self.cnt[e] for e in self.ENGS if self.cnt[e] > 0}
        snap.update({k: v for k, v in self.dcnt.items() if final or not k.startswith("d:w")})
        for e in engs:
            f = self.floor[e]
            for k, v in snap.items():
                if f.get(k, 0) < v:
                    f[k] = v

    def emit(self):
        nc = self.nc
        self.barrier(engs=("sp",), final=True)
        waits = self._deps("sp", (), ())
        self.q["sp"].append((waits, None, None))
        keys = list(self.ENGS) + list(self.dcnt.keys())
        with ExitStack() as st:
            sem = {}
            for i, k in enumerate(keys):
                sem[k] = st.enter_context(nc.semaphore("s%d" % i))
            block = st.enter_context(nc.Block())

            def run(ename, eobj):
                for waits, fn, inc in self.q[ename]:
                    for k, v in waits:
                        eobj.wait_ge(sem[k], v)
                    if fn is None:
                        continue
                    ins = fn(eobj)
                    if inc is not None:
                        ins.then_inc(sem[inc[0]], inc[1])

            block.tensor(lambda e: run("pe", e))
            block.scalar(lambda e: run("act", e))
            block.vector(lambda e: run("dve", e))
            block.gpsimd(lambda e: run("pool", e))
            block.sync(lambda e: run("sp", e))


def _prod(xs):
    r = 1
    for x in xs:
        r *= x
    return r


class Arena:
    def __init__(self, t_f32, nwords):
        self.t = t_f32
        self.nwords = nwords

    def view(self, off_bytes, shape, dtype):
        assert off_bytes % 4 == 0
        n = _prod(shape[1:])
        esz = 4 if dtype == F32 else 2
        nb = n * esz
        assert nb % 4 == 0
        assert off_bytes + nb <= self.nwords * 4, (off_bytes, nb, self.nwords * 4)
        base = self.t[0:shape[0], off_bytes // 4:(off_bytes + nb) // 4]
        ap = base if dtype == F32 else base.bitcast(dtype)
        if len(shape) == 3:
            ap = ap.rearrange("p (a b) -> p a b", a=shape[1])
        elif len(shape) == 4:
            ap = ap.rearrange("p (a b c) -> p a b c", a=shape[1], b=shape[2])
        return ap


def lay_lhsT(W):
    K, M = W.shape
    return np.ascontiguousarray(W.reshape(K // 128, 128, M // 128, 128).transpose(2, 1, 0, 3))


def lay_vec(v):
    return np.ascontiguousarray(v.reshape(-1, 128).T)


def small_layout():
    off = {}
    c = 0

    def add(name, n):
        nonlocal c
        off[name] = (c, n)
        c += n

    add("c", 8)
    add("flag", 1)
    for i in range(2):
        for j in range(4):
            add("ng%d_%d" % (i, j), 8)
        add("adab%d" % i, 48)
        add("fcw%d" % i, 44 * 3)
        add("fcb%d" % i, 44)
    add("rgcw", 8 * 4)
    add("rgcb", 8)
    add("rgba", 8)
    add("rgbx", 8)
    add("rglam", 8)
    add("glab", 4)
    add("glag", 2)
    return off, c


SM_OFF, NSMALL = small_layout()


def pack_small(inp, b, flag=1.0):
    sm = np.zeros((128, NSMALL), np.float32)

    def put(name, arr):
        o, n = SM_OFF[name]
        arr = np.asarray(arr, np.float32).reshape(128, n)
        sm[:, o:o + n] = arr

    put("c", lay_vec(inp["c"][b]))
    put("flag", np.full((128, 1), flag, np.float32))
    for i in range(2):
        for j in range(4):
            put("ng%d_%d" % (i, j), lay_vec(inp["norm_g"][i, j]))
        put("adab%d" % i, lay_vec(inp["ada_b"][i]))
        cw = inp["ffn_conv_w"][i]
        put("fcw%d" % i, np.stack([lay_vec(cw[k]) for k in range(3)], axis=-1))
        put("fcb%d" % i, lay_vec(inp["ffn_conv_b"][i]))
    rcw = inp["rg_conv_w"][0]
    put("rgcw", np.stack([lay_vec(rcw[k]) for k in range(4)], axis=-1))
    put("rgcb", lay_vec(inp["rg_conv_b"][0]))
    put("rgba", lay_vec(inp["rg_ba"][0]))
    put("rgbx", lay_vec(inp["rg_bx"][0]))
    put("rglam", lay_vec(inp["rg_lambda"][0]))
    put("glab", lay_vec(inp["gla_b_alpha"][0]))
    put("glag", lay_vec(inp["gla_norm_g"][0]))
    return sm


def prep_weights(inp):
    w = {}
    w["ada"] = np.stack([lay_lhsT(inp["ada_w"][i]) for i in range(2)])
    w["wup"] = np.stack([lay_lhsT(inp["ffn_w_up"][i]) for i in range(2)])
    w["wdn"] = np.stack([lay_lhsT(inp["ffn_w_down"][i]) for i in range(2)])
    w["rgin"] = lay_lhsT(inp["rg_w_in"][0])
    wa = inp["rg_wa"][0]
    wx = inp["rg_wx"][0]
    w["rgwa"] = np.concatenate([lay_lhsT(wa[g]) for g in range(4)], axis=0)
    w["rgwx"] = np.concatenate([lay_lhsT(wx[g]) for g in range(4)], axis=0)
    w["rgout"] = lay_lhsT(inp["rg_w_out"][0])
    gw = inp["gla_w_in"][0]
    w["glaqk"] = lay_lhsT(gw[:, 0:1024])
    w["glar"] = lay_lhsT(gw[:, 2048:3072])
    w["glav"] = np.ascontiguousarray(gw[:, 1024:2048].reshape(8, 128, 1024).transpose(1, 0, 2))
    w["glaz"] = np.ascontiguousarray(gw[:, 3072:3088].reshape(8, 128, 16).transpose(1, 0, 2))
    w["walpha"] = np.ascontiguousarray(inp["gla_w_alpha"][0])
    w["glaout"] = lay_lhsT(inp["gla_w_out"][0])
    return w


W_SHAPES = {
    "ada": [2, 48, 128, 8, 128], "wup": [2, 44, 128, 8, 128], "wdn": [2, 8, 128, 22, 128],
    "rgin": [16, 128, 8, 128], "rgwa": [8, 128, 2, 128], "rgwx": [8, 128, 2, 128],
    "rgout": [8, 128, 8, 128], "glaqk": [8, 128, 8, 128], "glar": [8, 128, 8, 128],
    "glav": [128, 8, 1024], "glaz": [128, 8, 16], "walpha": [16, 512], "glaout": [8, 128, 8, 128],
}


class Builder:
    def __init__(self, nseg, stages, npre=0):
        self.nseg = nseg
        self.npre = npre
        self.stages = stages
        self.nc = bass.Bass("TRN2", target_bir_lowering=False)
        self.POOL_M = (3, 7)
        self.sc = Sched(self.nc)

    def col(self, name, j=None, n=None):
        o, w = SM_OFF[name]
        if j is None:
            return self.small[:, o:o + w]
        return self.small[:, o + j:o + j + (1 if n is None else n)]

    def newc(self, name, n):
        o = self.cc_off
        self.cc_off += n
        assert self.cc_off <= self.cc_n
        self.cc_map[name] = (o, n)
        return self.cc[:, o:o + n]

    def cst(self, name, j=None):
        o, n = self.cc_map[name]
        if j is None:
            return self.cc[:, o:o + n]
        return self.cc[:, o + j:o + j + 1]

    def mm_bank(self):
        i = self.mm_i % 8
        self.mm_i += 1
        while i in getattr(self, "mm_excl", ()):
            i = self.mm_i % 8
            self.mm_i += 1
        return self.psall[:, i * 512:(i + 1) * 512], self.bank_res[i]

    def wload(self, src, shape):
        i = self.w_i % self.NW
        self.w_i += 1
        n = _prod(shape[1:])
        assert n <= self.WSLOT
        v = self.wring[0:shape[0], i, 0:n]
        if len(shape) == 3:
            v = v.rearrange("p (a b) -> p a b", a=shape[1])
        elif len(shape) == 4:
            v = v.rearrange("p (a b c) -> p a b c", a=shape[1], b=shape[2])
        res = self.w_res[i]
        self.sc.dma("pool", lambda e, v=v, src=src: e.dma_start(out=v, in_=src),
                    reads=(), writes=[res], key="w%d" % i)
        return v, res

    def mm_group(self, out_ap, out_res, pairs, reads):
        n = len(pairs)
        for i, (l, r) in enumerate(pairs):
            self.sc.op("pe",
                       lambda e, l=l, r=r, i=i: e.matmul(out_ap, l, r, start=(i == 0), stop=(i == n - 1)),
                       reads=reads if i == 0 else (), writes=[out_res] if i == 0 else (),
                       inc=(i == n - 1))

    def tmp(self, name, shape, dtype, nbuf):
        key = name
        if key not in self.tmps:
            n = _prod(shape[1:]) * (4 if dtype == F32 else 2)
            n = (n + 31) // 32 * 32
            bufs = []
            for b in range(nbuf):
                v = self.arena.view(self.tmp_off, shape, dtype)
                self.tmp_off += n
                assert self.tmp_off <= self.tmp_end, (name, self.tmp_off, self.tmp_end)
                bufs.append((v, Res("%s%d" % (name, b))))
            self.tmps[key] = [bufs, 0]
        ent = self.tmps[key]
        b = ent[0][ent[1] % len(ent[0])]
        ent[1] += 1
        return b

    def ntmp(self, name, shape, dtype, nbuf):
        if name not in self.ntmps:
            n = _prod(shape[1:]) * (4 if dtype == F32 else 2)
            n = (n + 31) // 32 * 32
            bufs = []
            for b in range(nbuf):
                v = self.arena.view(self.ntmp_off, shape, dtype)
                self.ntmp_off += n
                assert self.ntmp_off <= self.arena_bytes, (name, self.ntmp_off)
                bufs.append((v, Res("n%s%d" % (name, b))))
            self.ntmps[name] = [bufs, 0]
        ent = self.ntmps[name]
        b = ent[0][ent[1] % len(ent[0])]
        ent[1] += 1
        return b

    def reset_tmps(self, start, end):
        end = min(end, self.NT0)
        self.tmps = {}
        self.tmp_off = start
        self.tmp_end = end

    def rstd_from_sq(self, sq_aps, sq_res):
        ps, pres = self.mm_bank()
        self.mm_group(ps, pres, [(self.ones, a) for a in sq_aps], reads=list(sq_res))
        n = len(sq_aps) * 128
        rstd, rres = self.ntmp("rstd", [128, TT], F32, 2)
        self.sc.op("act", lambda e: e.activation(rstd, ps, AF.Ln, bias=self.cst("eps"), scale=1.0 / n),
                   reads=[pres], writes=[rres])
        self.sc.op("act", lambda e: e.activation(rstd, rstd, AF.Exp, scale=-0.5),
                   reads=[rres], writes=[rres])
        return rstd, rres

    def prenorm(self, gmod, shift, tts=None):
        sc = self.sc
        for tt in (range(NTT) if tts is None else tts):
            ts = slice(tt * TT, (tt + 1) * TT)
            sqs, sres = [], []
            for k in range(KT):
                sq, sqr = self.ntmp("sq", [128, TT], BF16, 8)
                x = self.xres[:, k, ts]
                sc.op("act", lambda e, sq=sq, x=x: e.activation(sq, x, AF.Square),
                      reads=[self.x_res[k][tt]], writes=[sqr])
                sqs.append(sq)
                sres.append(sqr)
            rstd, rres = self.rstd_from_sq(sqs, sres)
            for k in range(KT):
                t, tr = self.ntmp("nt", [128, TT], F32, 3)
                x = self.xres[:, k, ts]
                sc.op("dve", lambda e, t=t, x=x, rstd=rstd: e.tensor_tensor(t, x, rstd, ALU.mult),
                      reads=[self.x_res[k][tt], rres], writes=[tr])
                h = self.hT[:, k, ts]
                sc.op("act", lambda e, h=h, t=t, k=k: e.activation(
                    h, t, AF.Identity, bias=self.cst(shift, k), scale=self.cst(gmod, k)),
                    reads=[tr], writes=[self.h_res[k][tt]])

    def postnorm_residual(self, gg, tts=None, to_y=False, after_tt=None):
        sc = self.sc
        for tt in (range(NTT) if tts is None else tts):
            ts = slice(tt * TT, (tt + 1) * TT)
            rstd, rres = self.rstd_from_sq([self.ysq[:, m, ts] for m in range(KT)],
                                           [self.ysq_res[m][tt] for m in range(KT)])
            for m in range(KT):
                y = self.ymix[:, m, ts]
                x = self.xres[:, m, ts]
                eng = "pool" if m in self.POOL_M else "dve"
                sc.op(eng, lambda e, y=y, rstd=rstd: e.tensor_tensor(y, y, rstd, ALU.mult),
                      reads=[self.y_res[m][tt], rres], writes=[self.y_res[m][tt]])
                if to_y:
                    sc.op(eng, lambda e, y=y, x=x: e.tensor_tensor(y, x, y, ALU.add),
                          reads=[self.y_res[m][tt], self.x_res[m][tt]], writes=[self.y_res[m][tt]])
                else:
                    sc.op(eng, lambda e, y=y, x=x: e.tensor_tensor(x, x, y, ALU.add),
                          reads=[self.y_res[m][tt], self.x_res[m][tt]], writes=[self.x_res[m][tt]])
            if after_tt is not None:
                after_tt(tt)

    def outproj(self, wsrc, nk, act_ap, act_res, gg, fused=False, tts=None, to_y=False, after_tt=None):
        sc = self.sc

        def group(wv_m, wr, m, tt):
            ts = slice(tt * TT, (tt + 1) * TT)
            ps, pres = self.mm_bank()
            self.mm_group(ps, pres, [(wv_m[:, k, :], act_ap(k, ts)) for k in range(nk)],
                          reads=[wr] + [act_res(k, tt) for k in range(nk)])
            y = self.ymix[:, m, ts]
            q = self.ysq[:, m, ts]
            sc.op("act", lambda e, y=y, ps=ps, m=m: e.activation(y, ps, AF.Copy, scale=self.cst(gg, m)),
                  reads=[pres], writes=[self.y_res[m][tt]])
            sc.op("act", lambda e, q=q, ps=ps: e.activation(q, ps, AF.Square),
                  reads=[pres], writes=[self.ysq_res[m][tt]])

        if fused and nk == FT:
            ws = []
            for m in range(6):
                wv, wr = self.wload(wsrc[m], [128, nk, 128])
                ws.append((wv, [wr]))
            sc.barrier(engs=("pool",))
            ex0 = self.arena.view(94208, [128, nk, 128], BF16)
            ex0_res = Res("wex0")
            ex1 = self.arena.view(self.NT0, [128, nk, 128], BF16)
            ex1_res = [b[1] for b in self.ntmps["sq"][0][0:6]]
            sc.dma("pool", lambda e: e.dma_start(out=ex0, in_=wsrc[6]), writes=[ex0_res], key="wex0")
            sc.dma("pool", lambda e: e.dma_start(out=ex1, in_=wsrc[7]), writes=ex1_res, key="wex1")
            ws.append((ex0, [ex0_res]))
            ws.append((ex1, ex1_res))
            for tt in (range(NTT) if tts is None else tts):
                ts = slice(tt * TT, (tt + 1) * TT)
                for m in range(KT):
                    wv, wrl = ws[m]
                    ps, pres = self.mm_bank()
                    self.mm_group(ps, pres, [(wv[:, k, :], act_ap(k, ts)) for k in range(nk)],
                                  reads=list(wrl) + [act_res(k, tt) for k in range(nk)])
                    y = self.ymix[:, m, ts]
                    q = self.ysq[:, m, ts]
                    sc.op("act", lambda e, y=y, ps=ps, m=m: e.activation(y, ps, AF.Copy, scale=self.cst(gg, m)),
                          reads=[pres], writes=[self.y_res[m][tt]])
                    sc.op("act", lambda e, q=q, ps=ps: e.activation(q, ps, AF.Square),
                          reads=[pres], writes=[self.ysq_res[m][tt]])
                self.postnorm_residual(gg, tts=[tt], to_y=to_y, after_tt=after_tt)
            return
        if fused:
            ws = []
            for m2 in range(KT // 2):
                wv, wr = self.wload(wsrc[2 * m2:2 * m2 + 2].rearrange("c p k m -> p c k m"), [128, 2, nk, 128])
                ws.append((wv, wr))
            for tt in (range(NTT) if tts is None else tts):
                for m in range(KT):
                    wv, wr = ws[m // 2]
                    group(wv[:, m % 2], wr, m, tt)
                self.postnorm_residual(gg, tts=[tt])
            return
        for m in range(KT):
            wv, wr = self.wload(wsrc[m], [128, nk, 128])
            for tt in range(NTT):
                group(wv, wr, m, tt)

    def ffn(self, li, halo_only=False, final=False, after_tt=None):
        sc = self.sc
        A = self.arena
        self.hT = A.view(0, [128, KT, SEG], BF16)
        self.ysq = self.hT
        aT = A.view(16384, [128, FT, SEG], BF16)
        a_res = [[Res("a%d_%d" % (f, tt)) for tt in range(NTT)] for f in range(FT)]
        self.ymix = A.view(61440, [128, KT, SEG], F32)
        self.reset_tmps(61440, self.arena_bytes)
        self.prenorm("gmod_f%d" % li, "sh_f%d" % li, tts=([NTT - 1] if halo_only else None))
        sc.barrier(engs=("act", "dve"))
        wup = self.W["wup"]
        fcw = SM_OFF["fcw%d" % li][0]
        fcb = SM_OFF["fcb%d" % li][0]
        if halo_only:
            ts = slice(SEG - 128, SEG)
            for c in range(FT):
                wv, wr = self.wload(wup[li, c:c + FT + 1:FT].rearrange("c p k m -> p c k m"), [128, 2, KT, 128])
                for gv in range(2):
                    ch = c + gv * FT
                    ps, pres = self.mm_bank()
                    self.mm_group(ps[:, 0:128], pres, [(wv[:, gv, k, :], self.hT[:, k, ts]) for k in range(KT)],
                                  reads=[wr] + [self.h_res[k][NTT - 1] for k in range(KT)])
                    halo = self.fhalo[:, li, ch, :]
                    sc.op("act", lambda e, halo=halo, ps=ps: e.activation(halo, ps[:, 126:128], AF.Copy),
                          reads=[pres], writes=[self.fhalo_res[li][ch]])
            sc.barrier()
            return
        pend = []

        def f2_tail(c, accs):
            (ag, agr), (av, avr) = accs
            sc.op("act", lambda e, ag=ag: e.activation(ag, ag, AF.Gelu_apprx_tanh),
                  reads=[agr], writes=[agr])
            sc.op("dve", lambda e, ag=ag, av=av, c=c: e.tensor_tensor(aT[:, c, :], ag, av, ALU.mult),
                  reads=[agr, avr], writes=[a_res[c][0], a_res[c][1]])

        for c in range(FT):
            self.fill(1)
            wv, wr = self.wload(wup[li, c:c + FT + 1:FT].rearrange("c p k m -> p c k m"), [128, 2, KT, 128])
            accs = []
            for gv in range(2):
                ch = c + gv * FT
                ub, ur = self.tmp("ub", [128, SEG + 2], F32, 4)
                acc, ar = self.tmp("acc", [128, SEG], F32, 5)
                halo = self.fhalo[:, li, ch, :]
                hres = self.fhalo_res[li][ch]
                sc.op("act", lambda e, ub=ub, halo=halo: e.activation(ub[:, 0:2], halo, AF.Copy),
                      reads=[hres], writes=[ur])
                for tt in range(NTT):
                    ts = slice(tt * TT, (tt + 1) * TT)
                    ps, pres = self.mm_bank()
                    self.mm_group(ps, pres, [(wv[:, gv, k, :], self.hT[:, k, ts]) for k in range(KT)],
                                  reads=[wr] + [self.h_res[k][tt] for k in range(KT)])
                    sc.op("act", lambda e, ub=ub, ps=ps, tt=tt: e.activation(
                        ub[:, 2 + tt * TT:2 + (tt + 1) * TT], ps, AF.Copy),
                        reads=[pres], writes=[ur])
                    sc.op("act", lambda e, acc=acc, ps=ps, ts=ts, ch=ch: e.activation(
                        acc[:, ts], ps, AF.Identity,
                        bias=self.small[:, fcb + ch:fcb + ch + 1],
                        scale=self.small[:, fcw + ch * 3 + 2:fcw + ch * 3 + 3]),
                        reads=[pres], writes=[ar])
                sc.op("dve", lambda e, acc=acc, ub=ub, ch=ch: e.scalar_tensor_tensor(
                    acc, ub[:, 1:1 + SEG], self.small[:, fcw + ch * 3 + 1:fcw + ch * 3 + 2], acc,
                    ALU.mult, ALU.add), reads=[ur, ar], writes=[ar])
                sc.op("dve", lambda e, acc=acc, ub=ub, ch=ch: e.scalar_tensor_tensor(
                    acc, ub[:, 0:SEG], self.small[:, fcw + ch * 3:fcw + ch * 3 + 1], acc,
                    ALU.mult, ALU.add), reads=[ur, ar], writes=[ar])
                sc.op("dve", lambda e, ub=ub, halo=halo: e.tensor_copy(halo, ub[:, SEG:SEG + 2]),
                      reads=[ur], writes=[hres])
                accs.append((acc, ar))
            pend.append((c, accs))
            if len(pend) > 1:
                f2_tail(*pend.pop(0))
        while pend:
            f2_tail(*pend.pop(0))
        sc.barrier()
        self.reset_tmps(16384, 61440)
        self.outproj(self.W["wdn"][li], FT, lambda k, ts: aT[:, k, ts], lambda k, tt: a_res[k][tt],
                     gg="gg_f%d" % li, fused=True, to_y=final, after_tt=after_tt)

    def rglru(self):
        sc = self.sc
        A = self.arena
        self.hT = A.view(0, [128, KT, SEG], BF16)
        self.ysq = self.hT
        gy = A.view(16384, [128, KT, SEG], BF16)
        gy_res = [[Res("gy%d_%d" % (f, tt)) for tt in range(NTT)] for f in range(KT)]
        self.ymix = A.view(61440, [128, KT, SEG], F32)
        self.reset_tmps(32768, self.arena_bytes)
        self.prenorm("gmod_m0", "sh_m0")
        sc.barrier(engs=("act", "dve"))
        rgin = self.W["rgin"]
        def gate(f):
            wv, wr = self.wload(rgin[f], [128, KT, 128])
            for tt in range(NTT):
                ts = slice(tt * TT, (tt + 1) * TT)
                ps, pres = self.mm_bank()
                self.mm_group(ps, pres, [(wv[:, k, :], self.hT[:, k, ts]) for k in range(KT)],
                              reads=[wr] + [self.h_res[k][tt] for k in range(KT)])
                sc.op("act", lambda e, f=f, ts=ts, ps=ps: e.activation(gy[:, f, ts], ps, AF.Gelu_apprx_tanh),
                      reads=[pres], writes=[gy_res[f][tt]])

        cw = SM_OFF["rgcw"][0]

        def stageA(j):
            xcs = []
            for f in (2 * j, 2 * j + 1):
                wv, wr = self.wload(rgin[8 + f], [128, KT, 128])
                xb, xbr = self.tmp("xb", [128, SEG + 3], F32, 4)
                xc, xcr = self.tmp("xc", [128, SEG], F32, 2)
                xcb, xcbr = self.tmp("xcb", [128, SEG], BF16, 4)
                halo = self.rhalo[:, f, :]
                hres = self.rhalo_res[f]
                sc.op("act", lambda e, xb=xb, halo=halo: e.activation(xb[:, 0:3], halo, AF.Copy),
                      reads=[hres], writes=[xbr])
                for tt in range(NTT):
                    ts = slice(tt * TT, (tt + 1) * TT)
                    ps, pres = self.mm_bank()
                    self.mm_group(ps, pres, [(wv[:, k, :], self.hT[:, k, ts]) for k in range(KT)],
                                  reads=[wr] + [self.h_res[k][tt] for k in range(KT)])
                    sc.op("act", lambda e, xb=xb, ps=ps, tt=tt: e.activation(
                        xb[:, 3 + tt * TT:3 + (tt + 1) * TT], ps, AF.Copy),
                        reads=[pres], writes=[xbr])
                sc.op("act", lambda e, xb=xb, halo=halo: e.activation(halo, xb[:, SEG:SEG + 3], AF.Copy),
                      reads=[xbr], writes=[hres])
                sc.op("dve", lambda e, xc=xc, xb=xb, f=f: e.tensor_scalar(
                    xc, xb[:, 3:3 + SEG], self.small[:, cw + f * 4 + 3:cw + f * 4 + 4], self.col("rgcb", f),
                    ALU.mult, ALU.add), reads=[xbr], writes=[xcr])
                for kk in (2, 1):
                    sc.op("dve", lambda e, xc=xc, xb=xb, f=f, kk=kk: e.scalar_tensor_tensor(
                        xc, xb[:, kk:kk + SEG], self.small[:, cw + f * 4 + kk:cw + f * 4 + kk + 1], xc,
                        ALU.mult, ALU.add), reads=[xbr, xcr], writes=[xcr])
                sc.op("dve", lambda e, xc=xc, xcb=xcb, xb=xb, f=f: e.scalar_tensor_tensor(
                    xcb, xb[:, 0:SEG], self.small[:, cw + f * 4:cw + f * 4 + 1], xc,
                    ALU.mult, ALU.add), reads=[xbr, xcr], writes=[xcbr])
                xcs.append((xcb, xcbr))
            return xcs

        def stageB(j, xcs):
            wa, war = self.wload(self.W["rgwa"][2 * j:2 * j + 2].rearrange("c p k m -> p c k m"), [128, 2, 2, 128])
            wx, wxr = self.wload(self.W["rgwx"][2 * j:2 * j + 2].rearrange("c p k m -> p c k m"), [128, 2, 2, 128])
            ths = []
            for fi in range(2):
                f = 2 * j + fi
                tha, thar = self.tmp("tha", [128, SEG], F32, 2)
                thx, thxr = self.tmp("thx", [128, SEG], F32, 2)
                for (wv, wr, th, thr, bname) in ((wa, war, tha, thar, "hba"), (wx, wxr, thx, thxr, "hbx")):
                    for tt in range(NTT):
                        ts = slice(tt * TT, (tt + 1) * TT)
                        ps, pres = self.mm_bank()
                        self.mm_group(ps, pres, [(wv[:, fi, k, :], xcs[k][0][:, ts]) for k in range(2)],
                                      reads=[wr, xcs[0][1], xcs[1][1]])
                        sc.op("act", lambda e, th=th, ts=ts, ps=ps, bname=bname, f=f: e.activation(
                            th[:, ts], ps, AF.Tanh, bias=self.cst(bname, f), scale=0.5),
                            reads=[pres], writes=[thr])
                ths.append((tha, thar, thx, thxr))
            return ths

        def stageC(j, xcs, ths):
            bufs = []
            for fi in range(2):
                f = 2 * j + fi
                tha, thar, thx, thxr = ths[fi]
                av, avr = self.tmp("av", [128, SEG], F32, 2)
                a2, a2r = self.tmp("a2", [128, SEG], F32, 2)
                sc.op("act", lambda e, av=av, tha=tha, f=f: e.activation(
                    av, tha, AF.Exp, bias=self.cst("chalf", f), scale=self.cst("chalf", f)),
                    reads=[thar], writes=[avr])
                sc.op("act", lambda e, a2=a2, tha=tha, f=f: e.activation(
                    a2, tha, AF.Exp, bias=self.cst("clam", f), scale=self.cst("clam", f)),
                    reads=[thar], writes=[a2r])
                xcb, xcbr = xcs[fi]
                sc.op("dve", lambda e, thx=thx, xcb=xcb: e.scalar_tensor_tensor(
                    thx, thx, 1.0, xcb, ALU.add, ALU.mult), reads=[thxr, xcbr], writes=[thxr])
                bufs.append((av, avr, a2, a2r))
            for fi in range(2):
                av, avr, a2, a2r = bufs[fi]
                sc.op("dve", lambda e, a2=a2: e.tensor_scalar(a2, a2, 1.0, -1.0, ALU.min, ALU.mult),
                      reads=[a2r], writes=[a2r])
                sc.op("act", lambda e, a2=a2: e.activation(a2, a2, AF.Sqrt, bias=self.cst("one"), scale=1.0),
                      reads=[a2r], writes=[a2r])
            for fi in range(2):
                f = 2 * j + fi
                tha, thar, thx, thxr = ths[fi]
                av, avr, a2, a2r = bufs[fi]
                sc.op("dve", lambda e, thx=thx, a2=a2: e.scalar_tensor_tensor(
                    thx, thx, 0.5, a2, ALU.mult, ALU.mult), reads=[thxr, a2r], writes=[thxr])
                st = self.rstate[:, f:f + 1]
                sc.op("dve", lambda e, tha=tha, av=av, thx=thx, st=st: e.tensor_tensor_scan(
                    tha, av, thx, st, ALU.mult, ALU.add),
                    reads=[avr, thxr, self.rstate_res[f]], writes=[thar])
                sc.op("dve", lambda e, tha=tha, st=st: e.tensor_copy(st, tha[:, SEG - 1:SEG]),
                      reads=[thar], writes=[self.rstate_res[f]])
                sc.op("dve", lambda e, f=f, tha=tha: e.tensor_tensor(gy[:, f, :], gy[:, f, :], tha, ALU.mult),
                      reads=[thar, gy_res[f][0], gy_res[f][1]], writes=[gy_res[f][0], gy_res[f][1]])

        xcs_all = {}
        xcs_all[0] = stageA(0)
        gate(0)
        gate(1)
        for j in range(4):
            self.fill(1)
            if j + 1 < 4:
                xcs_all[j + 1] = stageA(j + 1)
            self.fill(1)
            ths = stageB(j, xcs_all[j])
            if j % 2 == 1:
                self.fill(1)
            stageC(j, xcs_all[j], ths)
            if j + 1 < 4:
                gate(2 * j + 2)
                gate(2 * j + 3)
        self.flush()
        sc.barrier()
        self.reset_tmps(32768, 61440)
        self.outproj(self.W["rgout"], KT, lambda k, ts: gy[:, k, ts], lambda k, tt: gy_res[k][tt], gg="gg_m0", fused=True)

    def gla(self, state_only=False, otts=None):
        sc = self.sc
        A = self.arena
        self.hT = A.view(0, [128, KT, SEG], BF16)
        self.ysq = self.hT
        qk = A.view(16384, [128, 8, SEG], BF16)
        qk_res = [[Res("qk%d_%d" % (f, tt)) for tt in range(NTT)] for f in range(8)]
        sr = A.view(32768, [128, KT, SEG], BF16)
        sr_res = [[Res("sr%d_%d" % (f, tt)) for tt in range(NTT)] for f in range(KT)]
        vtok = A.view(49152, [128, 8, D], BF16)
        v_res = [[Res("v%d_%d" % (j, h)) for h in range(4)] for j in range(8)]
        oT = A.view(65536, [128, KT, SEG], F32)
        o_res = [[Res("o%d_%d" % (f, j)) for j in range(8)] for f in range(KT)]
        self.ymix = A.view(61440, [128, KT, SEG], F32)
        T0 = 98304
        self.reset_tmps(65536, self.arena_bytes)
        self.prenorm("gmod_m1", "sh_m1")
        sc.barrier(engs=("act", "dve"))
        hres_all = lambda tt: [self.h_res[k][tt] for k in range(KT)]
        if otts is None:
            otts = list(range(NTT))
        if state_only:
            otts = []
        for f in range(0 if otts else KT, KT):
            wv, wr = self.wload(self.W["glar"][f], [128, KT, 128])
            for tt in otts:
                ts = slice(tt * TT, (tt + 1) * TT)
                ps, pres = self.mm_bank()
                self.mm_group(ps, pres, [(wv[:, k, :], self.hT[:, k, ts]) for k in range(KT)],
                              reads=[wr] + hres_all(tt))
                sc.op("act", lambda e, f=f, ts=ts, ps=ps: e.activation(sr[:, f, ts], ps, AF.Silu),
                      reads=[pres], writes=[sr_res[f][tt]])
        stop = int(os.environ.get("GLA_STOP", "9"))
        if stop <= 1:
            return
        wz, wzr = self.wload(self.W["glaz"], [128, KT, 16])
        zT, zr = self.tmp("zT", [16, SEG], BF16, 1)
        for tt in range(NTT):
            ts = slice(tt * TT, (tt + 1) * TT)
            ps, pres = self.mm_bank()
            self.mm_group(ps[0:16, :], pres, [(wz[:, k, :], self.hT[:, k, ts]) for k in range(KT)],
                          reads=[wzr] + hres_all(tt))
            sc.op("act", lambda e, ts=ts, ps=ps: e.activation(zT[:, ts], ps[0:16, :], AF.Copy),
                  reads=[pres], writes=[zr])
        if stop <= 2:
            return
        for h in range(4):
            lsp, lr = self.tmp("lsp", [128, SEG], F32, 1)
            Lc, Lr = self.tmp("Lc", [128, SEG], F32, 1)
            eG, eGr = self.tmp("eG", [128, SEG], F32, 1)
            enG, enGr = self.tmp("enG", [128, SEG], F32, 1)
            for tt in range(NTT):
                ts = slice(tt * TT, (tt + 1) * TT)
                ps, pres = self.mm_bank()
                self.mm_group(ps, pres, [(self.wal[:, h * 128:(h + 1) * 128], zT[:, ts])], reads=[zr, self.wal_res])
                sc.op("act", lambda e, lsp=lsp, ts=ts, ps=ps, h=h: e.activation(
                    lsp[:, ts], ps, AF.Exp, bias=self.cst("nglab", h), scale=-1.0),
                    reads=[pres], writes=[lr])
            sc.op("act", lambda e, lsp=lsp: e.activation(lsp, lsp, AF.Ln, bias=1.0),
                  reads=[lr], writes=[lr])
            sc.op("dve", lambda e, Lc=Lc, lsp=lsp: e.tensor_tensor_scan(
                Lc, self.rmask, lsp, 0.0, ALU.mult, ALU.add), reads=[lr, self.mask_res], writes=[Lr])
            sc.op("act", lambda e, eG=eG, Lc=Lc: e.activation(eG, Lc, AF.Exp, scale=-1.0 / 16.0),
                  reads=[Lr], writes=[eGr])
            sc.op("act", lambda e, enG=enG, Lc=Lc: e.activation(enG, Lc, AF.Exp, scale=1.0 / 16.0),
                  reads=[Lr], writes=[enGr])
            sc.op("dve", lambda e, eG=eG, h=h: e.tensor_copy(
                self.eGl[:, h, :], eG.rearrange("p (c j) -> p c j", j=128)[:, :, 127]),
                reads=[eGr], writes=[self.eGl_res[h]])
            if otts:
                wq, wqr = self.wload(self.W["glaqk"][h], [128, KT, 128])
            wk, wkr = self.wload(self.W["glaqk"][4 + h], [128, KT, 128])
            for tt in range(NTT):
                ts = slice(tt * TT, (tt + 1) * TT)
                if tt in otts:
                    ps, pres = self.mm_bank()
                    self.mm_group(ps, pres, [(wq[:, k, :], self.hT[:, k, ts]) for k in range(KT)],
                                  reads=[wqr] + hres_all(tt))
                    sc.op("dve", lambda e, h=h, ts=ts, ps=ps, eG=eG: e.scalar_tensor_tensor(
                        qk[:, h, ts], ps, 128.0 ** -0.5, eG[:, ts], ALU.mult, ALU.mult),
                        reads=[pres, eGr], writes=[qk_res[h][tt]])
                ps, pres = self.mm_bank()
                self.mm_group(ps, pres, [(wk[:, k, :], self.hT[:, k, ts]) for k in range(KT)],
                              reads=[wkr] + hres_all(tt))
                sc.op("dve", lambda e, h=h, ts=ts, ps=ps, enG=enG: e.tensor_tensor(
                    qk[:, 4 + h, ts], ps, enG[:, ts], ALU.mult),
                    reads=[pres, enGr], writes=[qk_res[4 + h][tt]])
        if stop <= 3:
            return
        for h in range(4):
            wv, wr = self.wload(self.W["glav"][:, :, h * 256:(h + 1) * 256], [128, KT, 256])
            for j in range(8):
                tcols = slice(j * 128, (j + 1) * 128)
                ps, pres = self.mm_bank()
                self.mm_group(ps[:, 0:256], pres, [(self.hT[:, k, tcols], wv[:, k, :]) for k in range(KT)],
                              reads=[wr] + hres_all(j // 4))
                eng = "act" if (j % 2 == 0) else "dve"
                if eng == "act":
                    sc.op("act", lambda e, j=j, h=h, ps=ps: e.activation(
                        vtok[:, j, h * 256:(h + 1) * 256], ps[:, 0:256], AF.Copy),
                        reads=[pres], writes=[v_res[j][h]])
                else:
                    sc.op("dve", lambda e, j=j, h=h, ps=ps: e.tensor_copy(
                        vtok[:, j, h * 256:(h + 1) * 256], ps[:, 0:256]),
                        reads=[pres], writes=[v_res[j][h]])
        if stop <= 4:
            return
        sc.barrier()
        self.reset_tmps(0, 16384)
        bank = lambda i: self.psall[:, i * 512:(i + 1) * 512]
        bA, bT, bO, bU = 4, 5, (6, 7), (0, 1)
        tbank = bank(bT).bitcast(BF16)
        stageA_all = []
        for j in range(8):
            tcols = slice(j * 128, (j + 1) * 128)
            tt = j // 4
            bAj = (bA, 2)[j % 2]
            bTj = (bT, 3)[j % 2]
            tbank = bank(bTj).bitcast(BF16)
            for h in range(4 if tt in otts else 0):
                self.mm_group(bank(bAj)[:, h * 128:(h + 1) * 128], self.bank_res[bAj],
                              [(qk[:, 4 + h, tcols], qk[:, h, tcols])],
                              reads=[qk_res[4 + h][tt], qk_res[h][tt]])
            for h in range(4):
                tp = tbank[:, h * 128:(h + 1) * 128]
                sc.op("pe", lambda e, tp=tp, h=h, tcols=tcols: e.transpose(tp, qk[:, 4 + h, tcols], self.ident),
                      reads=[qk_res[4 + h][tt], self.ident_res], writes=[self.bank_res[bTj]])
            stageA = []
            for h in range(4):
                asb, asr = self.tmp("asb", [128, 128], BF16, 32)
                aps = bank(bAj)[:, h * 128:(h + 1) * 128]
                if tt in otts:
                    sc.op("dve", lambda e, asb=asb, aps=aps: e.tensor_tensor(asb, aps, self.maskT, ALU.mult),
                          reads=[self.bank_res[bAj], self.mask_res], writes=[asr])
                ktok, ktr = self.tmp("ktok", [128, 128], BF16, 32)
                tp = tbank[:, h * 128:(h + 1) * 128]
                sc.op("act", lambda e, ktok=ktok, tp=tp: e.activation(ktok, tp, AF.Copy),
                      reads=[self.bank_res[bTj]], writes=[ktr])
                stageA.append((asb, asr, ktok, ktr))
            stageA_all.append(stageA)
        for j in range(8):
            tcols = slice(j * 128, (j + 1) * 128)
            tt = j // 4
            stageA = stageA_all[j]
            for h in range(4):
                asb, asr, ktok, ktr = stageA[h]
                ob = bO[h // 2]
                for e2 in range(2 if tt in otts else 0):
                    sl = (h % 2) * 2 + e2
                    ops = bank(ob)[:, sl * 128:(sl + 1) * 128]
                    ecols = slice(h * 256 + e2 * 128, h * 256 + (e2 + 1) * 128)
                    self.mm_group(ops, self.bank_res[ob],
                                  [(vtok[:, j, ecols], asb),
                                   (self.Sbf[:, h, e2 * 128:(e2 + 1) * 128], qk[:, h, tcols])],
                                  reads=[v_res[j][h], asr, self.Sbf_res[h], qk_res[h][tt]])
                ub = bU[h % 2]
                ups = bank(ub)[:, 0:256]
                self.mm_group(ups, self.bank_res[ub], [(ktok, vtok[:, j, h * 256:(h + 1) * 256])],
                              reads=[ktr, v_res[j][h]])
                dec = self.eGl[:, h, j:j + 1]
                Sh = self.S[:, h, :]
                sc.op("dve", lambda e, Sh=Sh, dec=dec: e.tensor_scalar(Sh, Sh, dec, None, ALU.mult),
                      reads=[self.S_res[h], self.eGl_res[h]], writes=[self.S_res[h]])
                sc.op("dve", lambda e, Sh=Sh, dec=dec, ups=ups: e.scalar_tensor_tensor(
                    Sh, ups, dec, Sh, ALU.mult, ALU.add),
                    reads=[self.bank_res[ub], self.S_res[h], self.eGl_res[h]], writes=[self.S_res[h]])
                sc.op("act", lambda e, Sh=Sh, h=h: e.activation(self.Sbf[:, h, :], Sh, AF.Copy),
                      reads=[self.S_res[h]], writes=[self.Sbf_res[h]])
            for h in range(4 if tt in otts else 0):
                ob = bO[h // 2]
                for e2 in range(2):
                    sl = (h % 2) * 2 + e2
                    ops = bank(ob)[:, sl * 128:(sl + 1) * 128]
                    f8 = 2 * h + e2
                    if e2 == 0:
                        sc.op("act", lambda e, f8=f8, tcols=tcols, ops=ops: e.activation(
                            oT[:, f8, tcols], ops, AF.Copy), reads=[self.bank_res[ob]], writes=[o_res[f8][j]])
                    else:
                        sc.op("dve", lambda e, f8=f8, tcols=tcols, ops=ops: e.tensor_copy(
                            oT[:, f8, tcols], ops), reads=[self.bank_res[ob]], writes=[o_res[f8][j]])
        sc.barrier()
        if stop <= 5 or state_only:
            return
        self.reset_tmps(T0, self.arena_bytes)
        for h in range(4):
            for tt in otts:
                ts = slice(tt * TT, (tt + 1) * TT)
                sqs, sres = [], []
                for e2 in range(2):
                    f8 = 2 * h + e2
                    sq, sqr = self.ntmp("osq", [128, TT], BF16, 4)
                    sc.op("act", lambda e, sq=sq, f8=f8, ts=ts: e.activation(sq, oT[:, f8, ts], AF.Square),
                          reads=[o_res[f8][jj] for jj in range(tt * 4, tt * 4 + 4)], writes=[sqr])
                    sqs.append(sq)
                    sres.append(sqr)
                rstd, rres = self.rstd_from_sq(sqs, sres)
                for e2 in range(2):
                    f8 = 2 * h + e2
                    o = oT[:, f8, ts]
                    ores = [o_res[f8][jj] for jj in range(tt * 4, tt * 4 + 4)]
                    sc.op("dve", lambda e, o=o, rstd=rstd: e.tensor_tensor(o, o, rstd, ALU.mult),
                          reads=ores + [rres], writes=ores)
                    sc.op("dve", lambda e, o=o, f8=f8, ts=ts, e2=e2: e.scalar_tensor_tensor(
                        sr[:, f8, ts], o, self.col("glag", e2), sr[:, f8, ts], ALU.mult, ALU.mult),
                        reads=ores + [sr_res[f8][tt]], writes=[sr_res[f8][tt]])
        sc.barrier()
        self.reset_tmps(T0, self.arena_bytes)
        self.outproj(self.W["glaout"], KT, lambda k, ts: sr[:, k, ts], lambda k, tt: sr_res[k][tt], gg="gg_m1", fused=True, tts=otts)

    def prologue(self):
        sc = self.sc
        nc = self.nc
        cres = Res("consts")
        self.cres = cres
        sc.dma("sp", lambda e: e.dma_start(out=self.small, in_=self.din["small"]), writes=[cres], key="small")
        self.mask_res = Res("masks")
        sc.dma("sp", lambda e: e.dma_start(out=self.maskT, in_=self.din["masks"][:, 0:128]),
               writes=[self.mask_res], key="masks")
        sc.dma("sp", lambda e: e.dma_start(out=self.rmask, in_=self.din["masks"][:, 128:128 + SEG]),
               writes=[self.mask_res], key="masks")
        idf, idr = self.arena.view(0, [128, 128], F32), Res("idf")
        sc.dma("sp", lambda e: e.dma_start(out=idf, in_=self.din["masks"][:, 128 + SEG:256 + SEG]),
               writes=[idr], key="idf")
        self.ident_res = Res("ident")
        sc.op("dve", lambda e: e.tensor_copy(self.ident, idf), reads=[idr], writes=[self.ident_res])
        waf, war = self.arena.view(1024, [16, 512], F32), Res("waf")
        sc.dma("sp", lambda e: e.dma_start(out=waf, in_=self.din["walpha"]), writes=[war], key="waf")
        self.wal_res = Res("wal")
        sc.op("dve", lambda e: e.tensor_copy(self.wal, waf), reads=[war], writes=[self.wal_res])
        epsc = self.newc("eps", 1)
        onec = self.newc("one", 1)
        sc.op("dve", lambda e: e.memset(epsc, EPS), writes=[cres])
        sc.op("dve", lambda e: e.memset(onec, 1.0), writes=[cres])
        ones_res = Res("ones")
        sc.op("dve", lambda e: e.memset(self.ones, 1.0), writes=[ones_res])
        for ap, res in ((self.fhalo_flat, self.fhalo_all), (self.rhalo_flat, self.rhalo_all),
                        (self.rstate, self.rstate_all), (self.S_flat, self.S_all), (self.Sbf_flat, self.Sbf_all)):
            sc.op("dve", lambda e, ap=ap: e.memset(ap, 0.0), writes=res)
        cact = self.cact
        cact_res = Res("cact")
        self.cact_res = cact_res
        sc.op("act", lambda e: e.activation(cact, self.col("c"), AF.Silu), reads=[cres], writes=[cact_res])
        t8 = self.newc("t8", 8)
        sc.op("act", lambda e: e.activation(t8, self.col("rglam"), AF.Exp, scale=-1.0), reads=[cres], writes=[cres])
        sc.op("act", lambda e: e.activation(t8, t8, AF.Ln, bias=1.0), reads=[cres], writes=[cres])
        clam = self.newc("clam", 8)
        chalf = self.newc("chalf", 8)
        hba = self.newc("hba", 8)
        hbx = self.newc("hbx", 8)
        ngl = self.newc("nglab", 4)
        sc.op("dve", lambda e: e.tensor_scalar(clam, t8, -8.0, None, ALU.mult), reads=[cres], writes=[cres])
        sc.op("dve", lambda e: e.tensor_scalar(chalf, t8, -4.0, None, ALU.mult), reads=[cres], writes=[cres])
        sc.op("dve", lambda e: e.tensor_scalar(hba, self.col("rgba"), 0.5, None, ALU.mult), reads=[cres], writes=[cres])
        sc.op("dve", lambda e: e.tensor_scalar(hbx, self.col("rgbx"), 0.5, None, ALU.mult), reads=[cres], writes=[cres])
        sc.op("dve", lambda e: e.tensor_scalar(ngl, self.col("glab"), -1.0, None, ALU.mult), reads=[cres], writes=[cres])
        sc.barrier()

    def mods_start(self, li, upfront=0):
        self.mm_excl = {7}
        mod = self.newc("mod%d" % li, 48)
        o = self.cc_map["mod%d" % li][0]
        self.cc_map["sh_m%d" % li] = (o, 8)
        self.cc_map["sh_f%d" % li] = (o + 24, 8)
        self.modstate = dict(li=li, items=list(range(16)), mod=mod,
                             gm=self.newc("gmod_m%d" % li, 8), ggm=self.newc("gg_m%d" % li, 8),
                             gf=self.newc("gmod_f%d" % li, 8), ggf=self.newc("gg_f%d" % li, 8))
        if upfront:
            self.fill(upfront)

    def fill(self, n=1):
        st = getattr(self, "modstate", None)
        if st is None:
            return False
        sc = self.sc
        cres = self.cres
        li = st["li"]
        mod = st["mod"]
        ps = self.psall[:, 7 * 512:8 * 512]
        pres = self.bank_res[7]
        for _ in range(n):
            if not st["items"]:
                break
            g = st["items"].pop(0)
            wv, wr = self.wload(self.W["ada"][li, 3 * g:3 * g + 3].rearrange("c p k m -> p c k m"),
                                [128, 3, KT, 128])
            for ci in range(3):
                m = 3 * g + ci
                self.mm_group(ps[:, m:m + 1], pres,
                              [(wv[:, ci, k, :], self.cact[:, k:k + 1]) for k in range(KT)],
                              reads=[wr, self.cact_res])
            if g == 5:
                sc.op("dve", lambda e, mod=mod, ps=ps, li=li: e.tensor_tensor(
                    mod[:, 0:18], ps[:, 0:18], self.col("adab%d" % li)[:, 0:18], ALU.add),
                    reads=[pres, cres], writes=[cres])
                gm = st["gm"]
                sc.op("dve", lambda e, gm=gm, mod=mod, li=li: e.scalar_tensor_tensor(
                    gm, mod[:, 8:16], 1.0, self.col("ng%d_0" % li), ALU.add, ALU.mult),
                    reads=[cres], writes=[cres])
        if not st["items"]:
            sc.op("dve", lambda e, mod=mod, ps=ps, li=li: e.tensor_tensor(
                mod[:, 18:48], ps[:, 18:48], self.col("adab%d" % li)[:, 18:48], ALU.add),
                reads=[pres, cres], writes=[cres])
            ggm, gf, ggf = st["ggm"], st["gf"], st["ggf"]
            sc.op("dve", lambda e, ggm=ggm, mod=mod, li=li: e.tensor_tensor(
                ggm, mod[:, 16:24], self.col("ng%d_1" % li), ALU.mult), reads=[cres], writes=[cres])
            sc.op("dve", lambda e, gf=gf, mod=mod, li=li: e.scalar_tensor_tensor(
                gf, mod[:, 32:40], 1.0, self.col("ng%d_2" % li), ALU.add, ALU.mult), reads=[cres], writes=[cres])
            sc.op("dve", lambda e, ggf=ggf, mod=mod, li=li: e.tensor_tensor(
                ggf, mod[:, 40:48], self.col("ng%d_3" % li), ALU.mult), reads=[cres], writes=[cres])
            self.modstate = None
            self.mm_excl = set()
        return True

    def flush(self):
        if getattr(self, "modstate", None) is not None:
            self.fill(100)
            self.sc.barrier()

    def build(self):
        nc = self.nc
        sc = self.sc
        self.din = {}
        self.din["xT"] = nc.dram_tensor("xT", [D, (self.npre + self.nseg) * SEG], F32, kind="ExternalInput").ap()
        self.din["small"] = nc.dram_tensor("small", [128, NSMALL], F32, kind="ExternalInput").ap()
        self.din["masks"] = nc.dram_tensor("masks", [128, 256 + SEG], F32, kind="ExternalInput").ap()
        self.din["walpha"] = nc.dram_tensor("walpha", [16, 512], F32, kind="ExternalInput").ap()
        self.W = {}
        for k, shp in W_SHAPES.items():
            if k == "walpha":
                continue
            self.W[k] = nc.dram_tensor(k, shp, F32, kind="ExternalInput").ap()
        outT = nc.dram_tensor("outT", [D, self.nseg * SEG], F32, kind="ExternalOutput").ap()
        self.NW = 6
        self.WSLOT = 3072
        self.arena_bytes = 122880
        with ExitStack() as st:
            def sb(name, shape, dt):
                return st.enter_context(nc.sbuf_tensor(name, shape, dt))
            xres_t = sb("xres", [128, KT * SEG], F32)
            arena_t = sb("arena", [128, self.arena_bytes // 4], F32)
            wring_t = sb("wring", [128, self.NW * self.WSLOT], BF16)
            small_t = sb("small_sb", [128, NSMALL], F32)
            cc_t = sb("cc", [128, 256], F32)
            ones_t = sb("ones", [128, 128], BF16)
            ident_t = sb("ident", [128, 128], BF16)
            maskT_t = sb("maskT", [128, 128], F32)
            rmask_t = sb("rmask", [128, SEG], F32)
            wal_t = sb("wal", [16, 512], BF16)
            cact_t = sb("cact", [128, 8], BF16)
            fhalo_t = sb("fhalo", [128, 2 * 44 * 2], F32)
            rhalo_t = sb("rhalo", [128, 8 * 3], F32)
            rstate_t = sb("rstate", [128, 8], F32)
            S_t = sb("Sst", [128, 4 * 256], F32)
            Sbf_t = sb("Sbf", [128, 4 * 256], BF16)
            eGl_t = sb("eGl", [128, 4 * 8], F32)
            ps_t = st.enter_context(nc.psum_tensor("psall", [128, 8 * 512], F32))

            self.xres = xres_t[:].rearrange("p (k t) -> p k t", k=KT)
            self.x_res = [[Res("x%d_%d" % (k, tt)) for tt in range(NTT)] for k in range(KT)]
            self.h_res = [[Res("h%d_%d" % (k, tt)) for tt in range(NTT)] for k in range(KT)]
            self.y_res = [[Res("y%d_%d" % (k, tt)) for tt in range(NTT)] for k in range(KT)]
            self.ysq_res = self.h_res
            self.arena = Arena(arena_t, self.arena_bytes // 4)
            self.wring = wring_t[:].rearrange("p (n w) -> p n w", n=self.NW)
            self.w_res = [Res("w%d" % i) for i in range(self.NW)]
            self.w_i = 0
            self.small = small_t[:]
            self.cc = cc_t[:]
            self.cc_off = 0
            self.cc_n = 256
            self.cc_map = {}
            self.ones = ones_t[:]
            self.ident = ident_t[:]
            self.maskT = maskT_t[:]
            self.rmask = rmask_t[:]
            self.wal = wal_t[:]
            self.cact = cact_t[:]
            self.fhalo_flat = fhalo_t[:]
            self.fhalo = fhalo_t[:].rearrange("p (l c k) -> p l c k", l=2, c=44)
            self.fhalo_res = [[Res("fh%d_%d" % (l, c)) for c in range(44)] for l in range(2)]
            self.fhalo_all = [r for l in self.fhalo_res for r in l]
            self.rhalo_flat = rhalo_t[:]
            self.rhalo = rhalo_t[:].rearrange("p (f k) -> p f k", f=8)
            self.rhalo_res = [Res("rh%d" % f) for f in range(8)]
            self.rhalo_all = self.rhalo_res
            self.rstate = rstate_t[:]
            self.rstate_res = [Res("rs%d" % f) for f in range(8)]
            self.rstate_all = self.rstate_res
            self.S_flat = S_t[:]
            self.S = S_t[:].rearrange("p (h e) -> p h e", h=4)
            self.S_res = [Res("S%d" % h) for h in range(4)]
            self.S_all = self.S_res
            self.Sbf_flat = Sbf_t[:]
            self.Sbf = Sbf_t[:].rearrange("p (h e) -> p h e", h=4)
            self.Sbf_res = [Res("Sb%d" % h) for h in range(4)]
            self.Sbf_all = self.Sbf_res
            self.eGl = eGl_t[:].rearrange("p (h c) -> p h c", h=4)
            self.eGl_res = [Res("eGl%d" % h) for h in range(4)]
            self.psall = ps_t[:]
            self.bank_res = [Res("bank%d" % i) for i in range(8)]
            self.mm_i = 0
            self.NT0 = self.arena_bytes - 22528
            self.ntmps = {}
            self.ntmp_off = self.NT0
            self.reset_tmps(65536, self.arena_bytes)

            self.prologue()
            self.mods_start(0, upfront=6)
            sc.barrier()
            allx = [r for l in self.x_res for r in l]
            preloaded = set()
            for seg in range(self.npre + self.nseg):
                pre = seg < self.npre
                last_pre = seg == self.npre - 1
                cols = slice(seg * SEG, (seg + 1) * SEG)
                def load_x(sg, tt):
                    c0 = sg * SEG + tt * TT
                    src = self.din["xT"][:, c0:c0 + TT].rearrange("(k p) t -> p k t", p=128)
                    dstx = self.xres[:, :, tt * TT:(tt + 1) * TT]
                    sc.dma("sp", lambda e, src=src, dstx=dstx: e.dma_start(out=dstx, in_=src),
                           writes=[self.x_res[k][tt] for k in range(KT)], key="xin%d" % tt)

                for tt in range(NTT):
                    if (seg, tt) not in preloaded:
                        load_x(seg, tt)
                nxt = None
                if seg + 1 < self.npre + self.nseg:
                    def nxt(tt, seg=seg, load_x=load_x):
                        load_x(seg + 1, tt)
                        preloaded.add((seg + 1, tt))
                if "m0" in self.stages:
                    self.rglru()
                self.flush()
                if seg == 0:
                    self.mods_start(1)
                if "f0" in self.stages:
                    self.ffn(0)
                self.flush()
                if "m1" in self.stages:
                    self.gla(state_only=(pre and not last_pre), otts=([NTT - 1] if last_pre else None))
                if "f1" in self.stages:
                    if not pre:
                        self.ffn(1, final=True, after_tt=nxt)
                    elif last_pre:
                        self.ffn(1, halo_only=True)
                if last_pre:
                    sc.barrier()
                    fl = self.col("flag")
                    for ap, res in ((self.fhalo_flat, self.fhalo_all), (self.rhalo_flat, self.rhalo_all),
                                    (self.rstate, self.rstate_all), (self.S_flat, self.S_all),
                                    (self.Sbf_flat, self.Sbf_all)):
                        sc.op("dve", lambda e, ap=ap, fl=fl: e.tensor_scalar(ap, ap, fl, None, ALU.mult),
                              reads=list(res), writes=list(res))
                    sc.barrier()
                if not pre:
                    final_in_y = "f1" in self.stages
                    for tt in range(NTT):
                        c0 = (seg - self.npre) * SEG + tt * TT
                        dst = outT[:, c0:c0 + TT].rearrange("(k p) t -> p k t", p=128)
                        if final_in_y:
                            srcy = self.ymix[:, :, tt * TT:(tt + 1) * TT]
                            rds = [self.y_res[k][tt] for k in range(KT)]
                        else:
                            srcy = self.xres[:, :, tt * TT:(tt + 1) * TT]
                            rds = [self.x_res[k][tt] for k in range(KT)]
                        sc.dma("sp", lambda e, dst=dst, srcy=srcy: e.dma_start(out=dst, in_=srcy),
                               reads=rds, key="xout%d" % tt)
            sc.emit()
        return nc


def make_masks():
    m = np.zeros((128, 256 + SEG), np.float32)
    j = np.arange(128)[:, None]
    i = np.arange(128)[None, :]
    m[:, 0:128] = (j <= i).astype(np.float32)
    t = np.arange(SEG)[None, :]
    m[:, 128:128 + SEG] = (t % 128 != 0).astype(np.float32)
    m[:, 128 + SEG:256 + SEG] = np.eye(128, dtype=np.float32)
    return m


_CACHE = {}


def run(inputs, nseg=2, npre=2, stages=("m0", "f0", "m1", "f1"), trace=False):
    inp = {k: np.asarray(v) for k, v in inputs.items()}
    key = (nseg, npre, tuple(stages))
    if key not in _CACHE:
        _CACHE[key] = Builder(nseg, stages, npre=npre).build()
    nc = _CACHE[key]
    w = prep_weights(inp)
    masks = make_masks()
    T = nseg * SEG
    P = npre * SEG
    in_maps = []
    for core in range(NCORES):
        b, half = core // 2, core % 2
        xb = inp["x"][b]
        real = xb[half * T:(half + 1) * T]
        prefix = xb[0:P] if half == 0 else xb[half * T - P:half * T]
        xT = np.ascontiguousarray(np.concatenate([prefix, real], axis=0).T)
        m = {"xT": xT, "small": pack_small(inp, b, flag=float(half)), "masks": masks}
        m.update(w)
        in_maps.append(m)
    res = run_bass_kernel_spmd(nc, in_maps, core_ids=list(range(NCORES)), trace=trace)
    out = np.empty((NB, 2 * T, D), np.float32)
    for core in range(NCORES):
        b, half = core // 2, core % 2
        out[b, half * T:(half + 1) * T] = res.results[core]["outT"].T
    return out, res


def kernel(**inputs):
    out, _ = run(inputs)
    return out
```
